# Optimizing a Trainium2 kernel written in Bass

```python
import jax
import jax.numpy as jnp
from jax import lax
import numpy as np

D_MODEL = 1024
BATCH = 8
SEQ = 8192
DEPTH = 2

GRID_W = 64
CTX_LEN = 256
EPS = 1e-6
N_MOD = 6

ATTN_HEADS = 8
ATTN_KV_HEADS = 2
HEAD_DIM = 64
AXIS_DIM = HEAD_DIM // 2
ROPE_THETA = 10000.0
Q_BLOCK = 128

SSM_HEADS = 8
SSM_HEAD_DIM = 64
SSM_D_INNER = SSM_HEADS * SSM_HEAD_DIM
SSM_GROUPS = 2
SSM_STATE = 64
SSM_CONV_W = 3
SSM_CHUNK = 128
SSM_CONV_DIM = SSM_D_INNER + 2 * SSM_GROUPS * SSM_STATE

Q_DIM = ATTN_HEADS * HEAD_DIM
KV_DIM = ATTN_KV_HEADS * HEAD_DIM
IDX_K = Q_DIM
IDX_V = IDX_K + KV_DIM
IDX_Z = IDX_V + KV_DIM
IDX_XBC = IDX_Z + SSM_D_INNER
IDX_DT = IDX_XBC + SSM_CONV_DIM
IN_PROJ_DIM = IDX_DT + 2 * SSM_HEADS
MIX_OUT_DIM = Q_DIM + SSM_D_INNER

SC_CONV_W = 3

FFN_HIDDEN = -(-(8 * D_MODEL) // (3 * 256)) * 256

N_EVEN = (DEPTH + 1) // 2
N_ODD = DEPTH // 2

kernel_name = "hybrid_gqa_ssd_shortconv_dit_block"


def rms_norm(x, g):
    xf = x.astype(jnp.float32)
    y = xf * lax.rsqrt(jnp.mean(xf * xf, axis=-1, keepdims=True) + EPS)
    return (y * g.astype(jnp.float32)).astype(x.dtype)


def modulate(h, shift, scale):
    return h * (1 + scale) + shift


def swiglu(h, w_gate, w_up, w_down):
    return (jax.nn.silu(h @ w_gate) * (h @ w_up)) @ w_down


def dw_conv(x, w, b=None):
    y = lax.conv_general_dilated(
        x, w[:, None, :].astype(x.dtype), window_strides=(1,), padding='SAME',
        dimension_numbers=('NWC', 'WIO', 'NWC'), feature_group_count=x.shape[-1])
    if b is not None:
        y = y + b
    return y


def axial_rope_tables(n_tokens):
    rows = n_tokens // GRID_W
    row = jnp.repeat(jnp.arange(rows), GRID_W).astype(jnp.float32)
    col = jnp.tile(jnp.arange(GRID_W), rows).astype(jnp.float32)
    inv = 1.0 / (ROPE_THETA ** (jnp.arange(0, AXIS_DIM, 2, dtype=jnp.float32) / AXIS_DIM))
    ang = jnp.concatenate([row[:, None] * inv, col[:, None] * inv], axis=-1)
    return jnp.cos(ang), jnp.sin(ang)


def apply_rope(x, cos, sin):
    xf = x.astype(jnp.float32).reshape(x.shape[:-1] + (HEAD_DIM // 2, 2))
    x0, x1 = xf[..., 0], xf[..., 1]
    cs, sn = cos[None, :, None, :], sin[None, :, None, :]
    out = jnp.stack([x0 * cs - x1 * sn, x0 * sn + x1 * cs], axis=-1)
    return out.reshape(x.shape).astype(x.dtype)


def attn_heads(p, q_norm, k_norm):
    b, t = p.shape[:2]
    q = rms_norm(p[..., :IDX_K].reshape(b, t, ATTN_HEADS, HEAD_DIM), q_norm)
    k = rms_norm(p[..., IDX_K:IDX_V].reshape(b, t, ATTN_KV_HEADS, HEAD_DIM), k_norm)
    v = p[..., IDX_V:IDX_Z].reshape(b, t, ATTN_KV_HEADS, HEAD_DIM)
    return q, k, v


def attend_blocks(q, keys, vals):
    b, t = q.shape[:2]
    grp = ATTN_HEADS // ATTN_KV_HEADS
    nb = t // Q_BLOCK
    qb = q.reshape(b, nb, Q_BLOCK, ATTN_KV_HEADS, grp, HEAD_DIM).transpose(1, 0, 2, 3, 4, 5)
    scale = HEAD_DIM ** -0.5

    def block(qi):
        s = jnp.einsum('bqkgd,bskd->bkgqs', qi, keys).astype(jnp.float32) * scale
        pr = jax.nn.softmax(s, axis=-1).astype(vals.dtype)
        return jnp.einsum('bkgqs,bskd->bqkgd', pr, vals)

    o = lax.map(block, qb)
    return o.transpose(1, 0, 2, 3, 4, 5).reshape(b, t, Q_DIM)


def segsum_exp(a):
    l = a.shape[-1]
    mask = jnp.tril(jnp.ones((l, l), dtype=bool))
    return jnp.exp(jnp.where(mask, a[..., :, None] - a[..., None, :], -jnp.inf))


def ssd_scan(xs, dt, a, bm, cm, h0):
    b, t, nh, p = xs.shape
    g, n = bm.shape[2], bm.shape[3]
    e = nh // g
    nc, l = t // SSM_CHUNK, SSM_CHUNK
    f32 = jnp.float32
    x = xs.astype(f32).reshape(b, nc, l, g, e, p)
    dtc = dt.astype(f32).reshape(b, nc, l, g, e)
    bc = bm.astype(f32).reshape(b, nc, l, g, n)
    cc = cm.astype(f32).reshape(b, nc, l, g, n)
    a_cs = jnp.cumsum(dtc * a.astype(f32).reshape(g, e), axis=2).transpose(0, 3, 4, 1, 2)
    xdt = x * dtc[..., None]
    cb = jnp.einsum('bclgn,bcsgn->bcgls', cc, bc)
    y_diag = jnp.einsum('bcgls,bgecls,bcsgep->bclgep', cb, segsum_exp(a_cs), xdt)
    decay_to_end = jnp.exp(a_cs[..., -1:] - a_cs)
    states = jnp.einsum('bclgn,bgecl,bclgep->bcgepn', bc, decay_to_end, xdt)
    chunk_decay = jnp.exp(a_cs[..., -1]).transpose(3, 0, 1, 2)

    def step(h, inp):
        dcy, st = inp
        return h * dcy[..., None, None] + st, h

    h_init = h0.astype(f32).reshape(b, g, e, p, n)
    h_final, h_in = lax.scan(step, h_init, (chunk_decay, states.transpose(1, 0, 2, 3, 4, 5)))
    y_off = jnp.einsum('bclgn,bgecl,cbgepn->bclgep', cc, jnp.exp(a_cs), h_in)
    y = (y_diag + y_off).reshape(b, t, nh, p)
    return y.astype(xs.dtype), h_final.reshape(b, nh, p, n)


def ssd_mixer(p, conv_w, conv_b, dt_bias, a_log, d_skip, ssm_norm, h0_f, h0_b):
    b, t = p.shape[:2]
    z = p[..., IDX_Z:IDX_XBC]
    xbc = jax.nn.silu(dw_conv(p[..., IDX_XBC:IDX_DT], conv_w, conv_b))
    xs = xbc[..., :SSM_D_INNER].reshape(b, t, SSM_HEADS, SSM_HEAD_DIM)
    bm = xbc[..., SSM_D_INNER:SSM_D_INNER + SSM_GROUPS * SSM_STATE].reshape(b, t, SSM_GROUPS, SSM_STATE)
    cm = xbc[..., SSM_D_INNER + SSM_GROUPS * SSM_STATE:].reshape(b, t, SSM_GROUPS, SSM_STATE)
    dt = jax.nn.softplus(p[..., IDX_DT:].astype(jnp.float32).reshape(b, t, 2, SSM_HEADS)
                         + dt_bias.astype(jnp.float32))
    a = -jnp.exp(a_log.astype(jnp.float32))
    y_f, h_f = ssd_scan(xs, dt[:, :, 0], a[0], bm, cm, h0_f)
    flip = lambda u: jnp.flip(u, axis=1)
    y_b, h_b = ssd_scan(flip(xs), flip(dt[:, :, 1]), a[1], flip(bm), flip(cm), h0_b)
    y = (y_f + flip(y_b) + xs * d_skip[:, None]).reshape(b, t, SSM_D_INNER)
    gy = (y.astype(jnp.float32) * jax.nn.silu(z.astype(jnp.float32))).reshape(b, t, SSM_GROUPS, -1)
    gy = gy * lax.rsqrt(jnp.mean(gy * gy, axis=-1, keepdims=True) + EPS)
    out = gy.reshape(b, t, SSM_D_INNER) * ssm_norm.astype(jnp.float32)
    return out.astype(p.dtype), h_f, h_b


def hybrid_mixer(h_lat, h_ctx, w_in, q_norm, k_norm, conv_w, conv_b, dt_bias, a_log, d_skip,
                 ssm_norm, w_out, cos, sin, ctx_out):
    p_lat = h_lat @ w_in
    p_ctx = h_ctx @ w_in
    qc, kc, vc = attn_heads(p_ctx, q_norm, k_norm)
    q, k, v = attn_heads(p_lat, q_norm, k_norm)
    q, k = apply_rope(q, cos, sin), apply_rope(k, cos, sin)
    a_lat = attend_blocks(q, jnp.concatenate([k, kc], axis=1), jnp.concatenate([v, vc], axis=1))
    b = h_lat.shape[0]
    zeros = jnp.zeros((b, SSM_HEADS, SSM_HEAD_DIM, SSM_STATE), jnp.float32)
    y_ctx, hf_ctx, hb_ctx = ssd_mixer(p_ctx, conv_w, conv_b, dt_bias, a_log, d_skip, ssm_norm, zeros, zeros)
    y_lat, _, _ = ssd_mixer(p_lat, conv_w, conv_b, dt_bias, a_log, d_skip, ssm_norm, hf_ctx, hb_ctx)
    m_lat = jnp.concatenate([a_lat, y_lat], axis=-1) @ w_out
    m_ctx = None
    if ctx_out:
        a_ctx = attend_blocks(qc, kc, vc)
        m_ctx = jnp.concatenate([a_ctx, y_ctx], axis=-1) @ w_out
    return m_lat, m_ctx


def shortconv_mixer(h, w_in, conv_w, w_out):
    gb, gc, u = jnp.split(h @ w_in, 3, axis=-1)
    return (gb * dw_conv(gc * u, conv_w)) @ w_out


def setup_inputs(seed: int = 0) -> dict:
    key = jax.random.key(seed)
    ks = iter(jax.random.split(key, 40))
    f32 = jnp.float32
    d = D_MODEL

    def nrm(shape, fan_in):
        return jax.random.normal(next(ks), shape, f32) * (fan_in ** -0.5)

    def gain(shape):
        return 1.0 + 0.05 * jax.random.normal(next(ks), shape, f32)

    x = jax.random.normal(next(ks), (BATCH, SEQ, d), f32)
    c = jax.random.normal(next(ks), (BATCH, d), f32)
    ctx = jax.random.normal(next(ks), (BATCH, CTX_LEN, d), f32)
    c_ctx = jax.random.normal(next(ks), (d,), f32)
    ada_w = 0.5 * nrm((DEPTH, d, N_MOD * d), d)
    ada_b = 0.02 * jax.random.normal(next(ks), (DEPTH, N_MOD * d), f32)
    norm_mix = gain((DEPTH, d))
    norm_ffn = gain((DEPTH, d))
    ffn_w_gate = nrm((DEPTH, d, FFN_HIDDEN), d)
    ffn_w_up = nrm((DEPTH, d, FFN_HIDDEN), d)
    ffn_w_down = nrm((DEPTH, FFN_HIDDEN, d), FFN_HIDDEN)
    hy_w_in = nrm((N_EVEN, d, IN_PROJ_DIM), d)
    hy_q_norm = gain((N_EVEN, HEAD_DIM))
    hy_k_norm = gain((N_EVEN, HEAD_DIM))
    hy_conv_w = nrm((N_EVEN, SSM_CONV_W, SSM_CONV_DIM), SSM_CONV_W)
    hy_conv_b = 0.02 * jax.random.normal(next(ks), (N_EVEN, SSM_CONV_DIM), f32)
    dt0 = jnp.exp(jax.random.uniform(next(ks), (N_EVEN, 2, SSM_HEADS), f32,
                                     np.log(1e-3).astype(np.float32), np.log(1e-1).astype(np.float32)))
    hy_dt_bias = dt0 + jnp.log(-jnp.expm1(-dt0))
    hy_a_log = jnp.log(jax.random.uniform(next(ks), (N_EVEN, 2, SSM_HEADS), f32, 1.0, 16.0))
    hy_d_skip = gain((N_EVEN, SSM_HEADS))
    hy_ssm_norm = gain((N_EVEN, SSM_D_INNER))
    hy_w_out = nrm((N_EVEN, MIX_OUT_DIM, d), MIX_OUT_DIM)
    sc_w_in = nrm((N_ODD, d, 3 * d), d)
    sc_conv_w = nrm((N_ODD, SC_CONV_W, d), SC_CONV_W)
    sc_w_out = nrm((N_ODD, d, d), d)
    final_norm = gain((d,))
    return {
        'x': x, 'c': c, 'ctx': ctx, 'c_ctx': c_ctx,
        'ada_w': ada_w, 'ada_b': ada_b, 'norm_mix': norm_mix, 'norm_ffn': norm_ffn,
        'ffn_w_gate': ffn_w_gate, 'ffn_w_up': ffn_w_up, 'ffn_w_down': ffn_w_down,
        'hy_w_in': hy_w_in, 'hy_q_norm': hy_q_norm, 'hy_k_norm': hy_k_norm,
        'hy_conv_w': hy_conv_w, 'hy_conv_b': hy_conv_b, 'hy_dt_bias': hy_dt_bias,
        'hy_a_log': hy_a_log, 'hy_d_skip': hy_d_skip, 'hy_ssm_norm': hy_ssm_norm, 'hy_w_out': hy_w_out,
        'sc_w_in': sc_w_in, 'sc_conv_w': sc_conv_w, 'sc_w_out': sc_w_out,
        'final_norm': final_norm,
    }


def reference(x, c, ctx, c_ctx, ada_w, ada_b, norm_mix, norm_ffn, ffn_w_gate, ffn_w_up, ffn_w_down,
              hy_w_in, hy_q_norm, hy_k_norm, hy_conv_w, hy_conv_b, hy_dt_bias, hy_a_log, hy_d_skip,
              hy_ssm_norm, hy_w_out, sc_w_in, sc_conv_w, sc_w_out, final_norm):
    cos, sin = axial_rope_tables(x.shape[1])
    h_ctx = ctx
    for i in range(DEPTH):
        ctx_read_later = any(j % 2 == 0 for j in range(i + 1, DEPTH))
        mod = (jax.nn.silu(c) @ ada_w[i] + ada_b[i])[:, None, :]
        sh1, sc1, g1, sh2, sc2, g2 = jnp.split(mod, N_MOD, axis=-1)
        hx = modulate(rms_norm(x, norm_mix[i]), sh1, sc1)
        need_ctx_in = (i % 2 == 0) or ctx_read_later
        if need_ctx_in:
            mod_c = (jax.nn.silu(c_ctx) @ ada_w[i] + ada_b[i])[None, None, :]
            csh1, csc1, cg1, csh2, csc2, cg2 = jnp.split(mod_c, N_MOD, axis=-1)
            hc = modulate(rms_norm(h_ctx, norm_mix[i]), csh1, csc1)
        if i % 2 == 0:
            e = i // 2
            m_lat, m_ctx = hybrid_mixer(hx, hc, hy_w_in[e], hy_q_norm[e], hy_k_norm[e], hy_conv_w[e],
                                        hy_conv_b[e], hy_dt_bias[e], hy_a_log[e], hy_d_skip[e],
                                        hy_ssm_norm[e], hy_w_out[e], cos, sin, ctx_read_later)
        else:
            o = i // 2
            m_lat = shortconv_mixer(hx, sc_w_in[o], sc_conv_w[o], sc_w_out[o])
            m_ctx = shortconv_mixer(hc, sc_w_in[o], sc_conv_w[o], sc_w_out[o]) if ctx_read_later else None
        x = x + g1 * m_lat
        x = x + g2 * swiglu(modulate(rms_norm(x, norm_ffn[i]), sh2, sc2),
                            ffn_w_gate[i], ffn_w_up[i], ffn_w_down[i])
        if ctx_read_later:
            h_ctx = h_ctx + cg1 * m_ctx
            h_ctx = h_ctx + cg2 * swiglu(modulate(rms_norm(h_ctx, norm_ffn[i]), csh2, csc2),
                                         ffn_w_gate[i], ffn_w_up[i], ffn_w_down[i])
    return rms_norm(x, final_norm)
```

```python
import contextlib
import math
import numpy as np
import concourse.bass as bass
import concourse.mybir as mybir
from concourse.bass_utils import run_bass_kernel_spmd

F32 = mybir.dt.float32
BF16 = mybir.dt.bfloat16
AF = mybir.ActivationFunctionType
ALU = mybir.AluOpType
EPS = 1e-6
D = 1024
KD = 8
CTX = 256
FH = 2816
NJ = 22
INP = 2064
ENGS = ('pe', 'act', 'dve', 'pool', 'sp')


class Tr:
    __slots__ = ('w', 'r')

    def __init__(self):
        self.w = None
        self.r = {}


class V:
    def __init__(self, buf, ap, part=None):
        self.buf = buf
        self.ap = ap
        self.part = part

    def __getitem__(self, k):
        return V(self.buf, self.ap[k], self.part)

    def rearrange(self, *a, **k):
        return V(self.buf, self.ap.rearrange(*a, **k), self.part)

    def to_broadcast(self, shape):
        return V(self.buf, self.ap.to_broadcast(shape), self.part)

    def unsqueeze(self, ax):
        return V(self.buf, self.ap.unsqueeze(ax), self.part)

    def partition_broadcast(self, n):
        return V(self.buf, self.ap.partition_broadcast(n), self.part)


class PV:
    def __init__(self, buf, key):
        self.buf = buf
        self.key = key

    def __getitem__(self, k):
        return V(self.buf, self.buf.t[k], self.key)


class Buf:
    def __init__(self, t, track=True):
        self.t = t
        self.track = track
        self.base = Tr()
        self.parts = {}

    def __getitem__(self, k):
        return V(self, self.t[k], None)

    def p(self, key):
        return PV(self, key)


class Sched:
    def __init__(self, nc, nds=32):
        self.nc = nc
        self.e = {'pe': nc.tensor, 'act': nc.scalar, 'dve': nc.vector, 'pool': nc.gpsimd, 'sp': nc.sync}
        self.sem = {k: nc.alloc_semaphore('sm_' + k) for k in ENGS}
        self.dsem = {}
        for q in ('sp', 'pool'):
            for i in range(nds):
                self.dsem[(q, i)] = nc.alloc_semaphore('sd_%s_%d' % (q, i))
        self.nds = nds
        self.cnt = {k: 0 for k in ENGS}
        self.dval = {k: 0 for k in self.dsem}
        self.ndma = {'sp': 0, 'pool': 0}
        self.waited = {k: {} for k in ENGS}
        self.ops = []
        self.n_ins = 0

    def _trs(self, v, write):
        b = v.buf
        if not b.track:
            return [], None
        if v.part is None:
            chk = [b.base] + list(b.parts.values())
            rec = b.base
        else:
            if v.part not in b.parts:
                b.parts[v.part] = Tr()
            chk = [b.base, b.parts[v.part]]
            rec = b.parts[v.part]
        return chk, rec

    def _record(self, eng, isdma, meth, kw):
        oid = len(self.ops)
        raw, oth = set(), set()
        recs_w, recs_r = [], []
        akw = {}
        for k, a in kw.items():
            if isinstance(a, V):
                akw[k] = a.ap
                w = k in ('out', 'accum_out', 'ap')
                chk, rec = self._trs(a, w)
                if rec is None:
                    continue
                for tr in chk:
                    if tr.w is not None:
                        (oth if w else raw).add(tr.w)
                    if w:
                        oth.update(tr.r.values())
                (recs_w if w else recs_r).append((a, rec))
            else:
                akw[k] = a
        deps = set(raw)
        for d in oth:
            o = self.ops[d]
            if (not isdma) and (not o['dma']) and o['eng'] == eng and eng == 'pe':
                continue
            deps.add(d)
        for d in deps:
            self.ops[d]['sig'] = True
        self.ops.append(dict(eng=eng, dma=isdma, meth=meth, kw=akw, deps=deps, sig=isdma, ev=None))
        for a, rec in recs_w:
            rec.w = oid
            rec.r = {}
            if a.part is None:
                for tr in a.buf.parts.values():
                    tr.w = None
                    tr.r = {}
        for a, rec in recs_r:
            rec.r[('d', oid) if isdma else eng] = oid
        return oid

    def op(self, eng, meth, **kw):
        return self._record(eng, False, meth, kw)

    def dma(self, q, **kw):
        return self._record(q, True, 'dma_start', kw)

    def _wait(self, eng, key, val):
        if self.waited[eng].get(key, 0) >= val:
            return
        sem = self.sem[key] if isinstance(key, str) else self.dsem[key]
        self.e[eng].wait_ge(sem, val)
        self.waited[eng][key] = val

    def flush(self):
        for o in self.ops:
            if o['ev'] is not None:
                continue
            eng = o['eng']
            for d in sorted(o['deps']):
                ev = self.ops[d]['ev']
                if ev is not None:
                    self._wait(eng, ev[0], ev[1])
            if o['dma']:
                i = (eng, self.ndma[eng] % self.nds)
                self.ndma[eng] += 1
                if self.dval[i] > 0:
                    self._wait(eng, i, self.dval[i])
                ins = self.e[eng].dma_start(**o['kw'])
                self.dval[i] += 16
                ins.then_inc(self.dsem[i], 16)
                o['ev'] = (i, self.dval[i])
            else:
                ins = getattr(self.e[eng], o['meth'])(**o['kw'])
                if o['sig']:
                    self.cnt[eng] += 1
                    ins.then_inc(self.sem[eng], 1)
                    o['ev'] = (eng, self.cnt[eng])
                else:
                    o['ev'] = ('none', 0)
            self.n_ins += 1
            o['kw'] = None

    def memset(self, eng, v, val):
        return self._record(eng, False, 'memset', {'ap': v, 'constant': val})

    def barrier(self, tile=None, name=None):
        if name is not None:
            with self.nc.named_scope(name):
                return self.barrier(tile, None)
        last = {}
        for o in self.ops:
            if o['ev'] is None and not o['dma']:
                last[o['eng']] = o
        for o in last.values():
            o['sig'] = True
        self.flush()
        for eng in ENGS:
            for k in ('pe', 'act', 'dve', 'pool'):
                self._wait(eng, k, self.cnt[k])
            for i in self.dsem:
                if self.dval[i] > 0:
                    self._wait(eng, i, self.dval[i])
        self.ops = []
        self._reset_trackers()

    def _reset_trackers(self):
        for b in self.allbufs:
            b.base = Tr()
            b.parts = {}

    allbufs = []


class _Stop(Exception):
    pass


def build(T, stop=None):
    NT = T // 512
    NCH = T // 128
    SK = T + CTX
    NKT = SK // 128
    nc = bass.Bass("TRN2", target_bir_lowering=False)
    S = Sched(nc)
    S.allbufs = []
    top = contextlib.ExitStack()

    def dram(name, shape, dt=F32, kind="ExternalInput"):
        return Buf(nc.dram_tensor(name, list(shape), dt, kind=kind).ap(), track=False)

    x_in = dram("x", [T, D])
    c_in = dram("c", [D])
    ctx_in = dram("ctx", [CTX, D])
    cctx_in = dram("c_ctx", [D])
    ada_w = dram("ada_w", [2, D, 6 * D])
    ada_b = dram("ada_b", [2, 6 * D])
    norm_mix = dram("norm_mix", [2, D])
    norm_ffn = dram("norm_ffn", [2, D])
    w_gate = dram("ffn_w_gate", [2, D, FH])
    w_up = dram("ffn_w_up", [2, D, FH])
    w_down = dram("ffn_w_down", [2, FH, D])
    hy_w_in = dram("hy_w_in", [1, D, INP])
    hy_q_norm = dram("hy_q_norm", [1, 64])
    hy_k_norm = dram("hy_k_norm", [1, 64])
    hy_conv_w = dram("hy_conv_w", [1, 3, 768])
    hy_conv_b = dram("hy_conv_b", [1, 768])
    hy_dt_bias = dram("hy_dt_bias", [1, 16])
    hy_a_log = dram("hy_a_log", [1, 16])
    hy_d_skip = dram("hy_d_skip", [1, 8])
    hy_ssm_norm = dram("hy_ssm_norm", [1, 512])
    hy_w_out = dram("hy_w_out", [1, D, D])
    sc_w_in = dram("sc_w_in", [1, D, 3 * D])
    sc_conv_w = dram("sc_conv_w", [1, 3, D])
    sc_w_out = dram("sc_w_out", [1, D, D])
    final_norm = dram("final_norm", [D])
    kc_in = dram("kconst", [128, 4])
    pos_in = dram("kpos", [128])
    out = dram("out", [T, D], kind="ExternalOutput")
    qT_d = dram("s_qT", [512, T], BF16, "Internal")
    zs_d = dram("s_zs", [T, 512], F32, "Internal")
    xbc_d = dram("s_xbc", [768, T], F32, "Internal")
    xbcc_d = dram("s_xbcc", [768, CTX], F32, "Internal")
    ysT_d = dram("s_ysT", [512, T], BF16, "Internal")
    gin_d = dram("s_gin", [NCH, 128, 256], BF16, "Internal")
    gin_d.track = True
    S.allbufs.append(gin_d)
    grow_d = dram("s_grow", [4, D], F32, "Internal")
    x1_d = dram("s_x1", [T, D], F32, "Internal")
    x2_d = dram("s_x2", [T, D], F32, "Internal")
    x3_d = dram("s_x3", [T, D], F32, "Internal")

    uid = [0]

    def sb(es, name, shape, dt=F32):
        uid[0] += 1
        b = Buf(es.enter_context(nc.sbuf_tensor("%s_%d" % (name, uid[0]), list(shape), dt)))
        S.allbufs.append(b)
        return b

    def ps(es, name, shape, dt=F32):
        uid[0] += 1
        b = Buf(es.enter_context(nc.psum_tensor("%s_%d" % (name, uid[0]), list(shape), dt)))
        S.allbufs.append(b)
        return b

    def chk(nm):
        if stop == nm:
            raise _Stop()

    def chkb(nm):
        if stop == nm:
            S.barrier(None)
            raise _Stop()

    try:
      with top:
        ident_f = sb(top, "ident_f", [128, 128])
        ident = sb(top, "ident", [128, 128], BF16)
        ones_f = sb(top, "ones_f", [128, 128])
        blk = sb(top, "blk", [128, 128], BF16)
        bsel = sb(top, "bsel", [128, 128])
        cols = sb(top, "cols", [128, 64])
        gcol = sb(top, "gcol", [128, 4])
        onec = sb(top, "onec", [128, 1])
        arow = sb(top, "arow", [128, 16])
        dskr = sb(top, "dskr", [128, 8])
        mark = {'act': sb(top, "mk_a", [1, 4]), 'dve': sb(top, "mk_d", [1, 4]), 'pool': sb(top, "mk_p", [1, 4]),
                'pe_sb': sb(top, "mk_pe", [1, 4], BF16)}
        colsc = sb(top, "colsc", [128, 16])
        mid = contextlib.ExitStack()
        mk = {n: sb(mid, "mk_" + n, [128, 128]) for n in ('le', 'gt', 'ge', 'lt')}
        ropeT = sb(mid, "ropeT", [128, 4, 128])
        dt_all = sb(mid, "dt_all", [128, NCH + 2, 16])
        KT = sb(mid, "KT", [128, SK], BF16)
        Vt = sb(mid, "Vt", [128, NKT, 2, 65], BF16)

        S.memset('pool', ident_f[:], 0.0)
        S.op('pool', 'affine_select', out=ident_f[:], in_=ident_f[:], pattern=[[-1, 128]], compare_op=ALU.not_equal,
             fill=1.0, base=0, channel_multiplier=1)
        S.op('dve', 'tensor_copy', out=ident[:], in_=ident_f[:])
        S.memset('pool', ones_f[:], 1.0)
        for n, cmp_, sg_ in (('le', ALU.is_ge, -1), ('gt', ALU.is_gt, 1), ('ge', ALU.is_ge, 1), ('lt', ALU.is_gt, -1)):
            S.op('pool', 'affine_select', out=mk[n][:], in_=ones_f[:], pattern=[[-sg_, 128]], compare_op=cmp_,
                 fill=0.0, base=0, channel_multiplier=sg_)
        S.memset('pool', blk[:], 0.0)
        S.memset('pool', blk[0:64, 0:64], 1.0 / 64)
        S.memset('pool', blk[64:128, 64:128], 1.0 / 64)
        S.memset('pool', bsel[:], 0.0)
        S.memset('pool', bsel[64:65, 0:64], 1.0)
        S.memset('pool', mark['pe_sb'][:], 0.0)
        S.memset('pool', onec[:], 1.0)
        for e_ in ('act', 'dve', 'pool'):
            S.memset('pool', mark[e_][:], 0.0)

        with contextlib.ExitStack() as es:
            ccol = sb(es, "ccol", [128, 2, 8])
            scb = sb(es, "scb", [128, 2, 8, 128])
            awt = [sb(es, "awt%d" % i, [128, 8, 1024]) for i in range(2)]
            brow = [sb(es, "brow%d" % i, [128, 1024]) for i in range(2)]
            nrow = sb(es, "nrow", [128, 4, 8])
            Rt = sb(es, "Rt", [128, 1024])
            tmpd = sb(es, "tmpd", [128, 1024])
            psR = [ps(es, "psR%d" % i, [128, 512]) for i in range(2)]
            S.dma('sp', out=ccol[:, 0, :], in_=c_in[:].rearrange("(k p) -> p k", p=128), allow_slow_non_contiguous=True)
            S.dma('sp', out=ccol[:, 1, :], in_=cctx_in[:].rearrange("(k p) -> p k", p=128), allow_slow_non_contiguous=True)
            for l in range(2):
                S.dma('sp', out=nrow[:, 2 * l, :], in_=norm_mix[l, :].rearrange("(k p) -> p k", p=128),
                      allow_slow_non_contiguous=True)
                S.dma('sp', out=nrow[:, 2 * l + 1, :], in_=norm_ffn[l, :].rearrange("(k p) -> p k", p=128),
                      allow_slow_non_contiguous=True)
            S.op('act', 'activation', out=ccol[:], in_=ccol[:], func=AF.Silu)
            for v_ in range(2):
                for k in range(8):
                    S.op('dve', 'tensor_copy', out=scb[:, v_, k, :], in_=ccol[:, v_, k:k + 1].to_broadcast([128, 128]))
            gi = 0
            jobs = [(0, 0, g) for g in range(6)] + [(0, 1, 0), (0, 1, 1)] + [(1, 0, g) for g in range(6)]
            for (l, v_, g) in jobs:
                a = awt[gi % 2]
                br = brow[gi % 2]
                gi += 1
                for k in range(8):
                    S.dma('sp', out=a.p(k)[:, k, :], in_=ada_w[l, k * 128:(k + 1) * 128, g * 1024:(g + 1) * 1024])
                S.dma('sp', out=br[:], in_=ada_b[l, g * 1024:(g + 1) * 1024].partition_broadcast(128))
                for h in range(2):
                    for k in range(8):
                        S.op('pe', 'matmul', out=psR[h][:], lhsT=scb[:, v_, k, :], rhs=a.p(k)[:, k, h * 512:(h + 1) * 512],
                             start=(k == 0), stop=(k == 7))
                    S.op('dve', 'tensor_tensor', out=Rt[:, h * 512:(h + 1) * 512], in0=psR[h][:],
                         in1=br[:, h * 512:(h + 1) * 512], op=ALU.add)
                if g in (2, 5) and v_ == 0:
                    S.dma('sp', out=grow_d[l * 2 + (0 if g == 2 else 1), :], in_=Rt[0:1, :])
                else:
                    S.op('dve', 'tensor_tensor', out=tmpd[:].rearrange("p (k f) -> p k f", k=8),
                         in0=Rt[:].rearrange("p (k f) -> p k f", k=8),
                         in1=ident_f[:].unsqueeze(1).to_broadcast([128, 8, 128]), op=ALU.mult)
                    if v_ == 1:
                        dst = colsc[:, g * 8:(g + 1) * 8]
                    else:
                        off = {0: 8, 1: 0, 3: 24, 4: 16}[g]
                        dst = cols[:, l * 32 + off: l * 32 + off + 8]
                    S.op('dve', 'tensor_reduce', out=dst, in_=tmpd[:].rearrange("p (k f) -> p k f", k=8),
                         axis=mybir.AxisListType.X, op=ALU.add)
                    if g in (1, 4):
                        nr = nrow[:, 2 * l + (0 if g == 1 else 1), :]
                        S.op('dve', 'scalar_tensor_tensor', out=dst, in0=dst, scalar=1.0, in1=nr, op0=ALU.add, op1=ALU.mult)
            for j, src in ((0, hy_q_norm), (2, hy_k_norm)):
                for half in range(2):
                    S.dma('sp', out=gcol[half * 64:(half + 1) * 64, j:j + 1], in_=src[0, :].rearrange("(d o) -> d o", o=1),
                          allow_slow_non_contiguous=True)
                    pr = src[0, :].rearrange("(i two) -> i two", two=2)
                    S.dma('sp', out=gcol[half * 64:(half + 1) * 64:2, j + 1:j + 2], in_=pr[:, 1:2],
                          allow_slow_non_contiguous=True)
                    S.dma('sp', out=gcol[half * 64 + 1:(half + 1) * 64:2, j + 1:j + 2], in_=pr[:, 0:1],
                          allow_slow_non_contiguous=True)
            kc = sb(es, "kc", [128, 4])
            posb = sb(es, "posb", [128, 128])
            ang = sb(es, "ang", [128, 128])
            S.dma('sp', out=kc[:], in_=kc_in[:])
            S.dma('sp', out=posb[:], in_=pos_in[:].partition_broadcast(128))
            angi = sb(es, "angi", [128, 128], mybir.dt.int32)
            angk = sb(es, "angk", [128, 128])
            angm = sb(es, "angm", [128, 128])
            for ti, (fcol, sh) in enumerate(((0, 0.5 * math.pi), (0, 0.0), (1, 0.5 * math.pi), (1, 0.0))):
                S.op('dve', 'tensor_scalar', out=ang[:], in0=posb[:], scalar1=kc[:, fcol:fcol + 1], scalar2=sh,
                     op0=ALU.mult, op1=ALU.add)
                S.op('dve', 'tensor_scalar', out=angk[:], in0=ang[:], scalar1=1.0 / (2 * math.pi), scalar2=None, op0=ALU.mult)
                S.op('dve', 'tensor_copy', out=angi[:], in_=angk[:])
                S.op('dve', 'tensor_copy', out=angk[:], in_=angi[:])
                S.op('dve', 'scalar_tensor_tensor', out=ang[:], in0=angk[:], scalar=-2 * math.pi, in1=ang[:], op0=ALU.mult,
                     op1=ALU.add)
                S.op('dve', 'tensor_scalar', out=angm[:], in0=ang[:], scalar1=math.pi, scalar2=-2 * math.pi, op0=ALU.is_gt,
                     op1=ALU.mult)
                S.op('dve', 'tensor_tensor', out=ang[:], in0=ang[:], in1=angm[:], op=ALU.add)
                S.op('dve', 'tensor_scalar', out=angm[:], in0=ang[:], scalar1=-1.0, scalar2=math.pi, op0=ALU.mult,
                     op1=ALU.is_gt)
                S.op('dve', 'scalar_tensor_tensor', out=ang[:], in0=angm[:], scalar=2 * math.pi, in1=ang[:], op0=ALU.mult,
                     op1=ALU.add)
                S.op('act', 'activation', out=ropeT[:, ti, :], in_=ang[:], func=AF.Sin)
            for ti in (1, 3):
                S.op('dve', 'tensor_scalar', out=ropeT[:, ti, :], in0=ropeT[:, ti, :], scalar1=kc[:, 2:3], scalar2=None,
                     op0=ALU.mult)
            S.dma('sp', out=arow[:], in_=hy_a_log[0, :].partition_broadcast(128))
            S.op('act', 'activation', out=arow[:], in_=arow[:], func=AF.Exp)
            S.op('dve', 'tensor_scalar', out=arow[:], in0=arow[:], scalar1=-1.0, scalar2=None, op0=ALU.mult)
            S.dma('sp', out=dskr[:], in_=hy_d_skip[0, :].partition_broadcast(128))
            S.barrier(mark, 'ph_P0'); chk('P0')

        def fe_load_sub(es_bufs, src_d, t0, s):
            xs2, hn, hxT, ss, rstd, psTt = es_bufs
            xs_ = xs2[s % 2]
            S.dma('sp', out=xs_[:], in_=src_d[t0 + s * 128:t0 + (s + 1) * 128, :])
            S.memset('pool', ss[:, s:s + 1], 0.0)
            S.op('act', 'activation', out=hn[:, s, :], in_=xs_[:], func=AF.Square, accum_out=ss[:, s:s + 1])
            S.op('dve', 'tensor_scalar', out=rstd[:, s:s + 1], in0=ss[:, s:s + 1], scalar1=1.0 / D, scalar2=EPS,
                 op0=ALU.mult, op1=ALU.add)
            S.op('act', 'activation', out=rstd[:, s:s + 1], in_=rstd[:, s:s + 1], func=AF.Sqrt)
            S.op('dve', 'reciprocal', out=rstd[:, s:s + 1], in_=rstd[:, s:s + 1])
            if s % 2 == 0:
                S.op('dve', 'tensor_scalar', out=hn[:, s, :], in0=xs_[:], scalar1=rstd[:, s:s + 1], scalar2=None,
                     op0=ALU.mult)
            else:
                S.op('act', 'activation', out=hn[:, s, :], in_=xs_[:], func=AF.Identity, scale=rstd[:, s:s + 1])

        def fe_load(es_bufs, src_d, t0, ns):
            for s in range(ns):
                fe_load_sub(es_bufs, src_d, t0, s)

        def fe_trans(es_bufs, ns, Acol, Bcol, par):
            xs2, hn, hxT, ss, rstd, psTt = es_bufs
            hx_ = hxT[par % len(hxT)]
            for k in range(8):
                pt = psTt[k % len(psTt)]
                for s in range(ns):
                    S.op('pe', 'transpose', out=pt[:, s * 128:(s + 1) * 128], in_=hn[:, s, k * 128:(k + 1) * 128],
                         identity=ident[:])
                S.op('act', 'activation', out=hx_[:, k, 0:ns * 128], in_=pt[:, 0:ns * 128], func=AF.Identity,
                     bias=Bcol[:, k:k + 1], scale=Acol[:, k:k + 1])
            return hx_

        def front_end(es_bufs, src_d, t0, ns, Acol, Bcol, par):
            fe_load(es_bufs, src_d, t0, ns)
            return fe_trans(es_bufs, ns, Acol, Bcol, par)

        def fe_bufs(es, nsmax=4, nhx=2, npt=2):
            xs2 = [sb(es, "xs%d" % i, [128, D]) for i in range(2)]
            hn = sb(es, "hn", [128, nsmax, D], BF16)
            hxT = [sb(es, "hxT%d" % i, [128, 8, nsmax * 128], BF16) for i in range(nhx)]
            ss = sb(es, "ss", [128, 4])
            rstd = sb(es, "rstd", [128, 4])
            psTt = [ps(es, "psTt%d" % i, [128, 1024], BF16) for i in range(npt)]
            return (xs2, hn, hxT, ss, rstd, psTt)

        with mid:
            S.memset('pool', Vt[:], 1.0)

            with contextlib.ExitStack() as es:
                win = sb(es, "win", [128, 8, INP], BF16)
                wsw = sb(es, "wsw", [128, 8, 640], BF16)
                wstage = [sb(es, "wstage%d" % i, [128, 640]) for i in range(2)]
                for k in range(8):
                    S.dma('pool', out=win.p(k)[:, k, :], in_=hy_w_in[0, k * 128:(k + 1) * 128, :])
                    st = wstage[k % 2]
                    S.dma('sp', out=st[:], in_=hy_w_in[0, k * 128:(k + 1) * 128, 0:640])
                    S.op('dve', 'tensor_copy', out=wsw.p(k)[:, k, 0:640:2], in_=st[:, 1:640:2])
                    S.op('dve', 'tensor_copy', out=wsw.p(k)[:, k, 1:640:2], in_=st[:, 0:640:2])
                dtb = sb(es, "dtb", [128, 16])
                S.dma('sp', out=dtb[:], in_=hy_dt_bias[0, :].partition_broadcast(128))
                feb = fe_bufs(es)
                psA = [ps(es, "psA%d" % i, [128, 512]) for i in range(4)]
                psM = ps(es, "psMs", [128, 512])
                sq = sb(es, "sq", [128, 512], BF16)
                rs = sb(es, "rs", [128, 512])
                ta = sb(es, "ta", [128, 512])
                tb = sb(es, "tb", [128, 512])
                Ct = sb(es, "Ct", [128, 512])
                St = sb(es, "St", [128, 512])
                qst = [sb(es, "qst%d" % i, [128, 4, 512], BF16) for i in range(2)]
                zst = [sb(es, "zst0", [128, 4, 512])] * 2
                xst = [sb(es, "xst0", [128, 6, 512])] * 2
                epsc = sb(es, "epsc", [128, 1])
                S.memset('pool', epsc[:], EPS)

                def qk_chunk(hx_, n, col0, gi_, dst, rope):
                    pp, psw = psA[0], psA[1]
                    for k in range(8):
                        S.op('pe', 'matmul', out=pp[:, 0:n], lhsT=win[:, k, col0:col0 + 128], rhs=hx_[:, k, 0:n],
                             start=(k == 0), stop=(k == 7))
                    if rope:
                        for k in range(8):
                            S.op('pe', 'matmul', out=psw[:, 0:n], lhsT=wsw[:, k, col0:col0 + 128], rhs=hx_[:, k, 0:n],
                                 start=(k == 0), stop=(k == 7))
                    S.op('act', 'activation', out=sq[:, 0:n], in_=pp[:, 0:n], func=AF.Square)
                    S.op('pe', 'matmul', out=psM[:, 0:n], lhsT=blk[:], rhs=sq[:, 0:n], start=True, stop=True)
                    S.op('act', 'activation', out=rs[:, 0:n], in_=psM[:, 0:n], func=AF.Sqrt, bias=epsc[:, 0:1], scale=1.0)
                    S.op('dve', 'reciprocal', out=rs[:, 0:n], in_=rs[:, 0:n])
                    S.op('act', 'activation', out=ta[:, 0:n], in_=pp[:, 0:n], func=AF.Identity, scale=gcol[:, gi_:gi_ + 1])
                    if rope:
                        S.op('act', 'activation', out=tb[:, 0:n], in_=psw[:, 0:n], func=AF.Identity,
                             scale=gcol[:, gi_ + 1:gi_ + 2])
                        S.op('pool', 'tensor_tensor', out=ta[:, 0:n], in0=ta[:, 0:n], in1=Ct[:, 0:n], op=ALU.mult)
                        S.op('pool', 'tensor_tensor', out=tb[:, 0:n], in0=tb[:, 0:n], in1=St[:, 0:n], op=ALU.mult)
                        S.op('dve', 'tensor_tensor', out=ta[:, 0:n], in0=ta[:, 0:n], in1=tb[:, 0:n], op=ALU.add)
                    S.op('dve', 'tensor_tensor', out=dst, in0=ta[:, 0:n], in1=rs[:, 0:n], op=ALU.mult)

                tiles = [('ctx', 0)] + [('lat', i) for i in range(NT)]
                for ti, (kind, i) in enumerate(tiles):
                    par = ti % 2
                    lat = kind == 'lat'
                    ns = 4 if lat else 2
                    n = ns * 128
                    t0 = i * 512
                    if lat:
                        hx_ = front_end(feb, x_in, t0, ns, cols[:, 0:8], cols[:, 8:16], par)
                    else:
                        hx_ = front_end(feb, ctx_in, 0, ns, colsc[:, 8:16], colsc[:, 0:8], par)
                    key0 = t0 if lat else T
                    kt0 = key0 // 128
                    if lat:
                        r0 = t0 // 64
                        S.op('pool', 'tensor_tensor', out=Ct[:].rearrange("p (r c) -> p r c", c=64),
                             in0=ropeT[:, 0, r0:r0 + 8].unsqueeze(2).to_broadcast([128, 8, 64]),
                             in1=ropeT[:, 2, 0:64].unsqueeze(1).to_broadcast([128, 8, 64]), op=ALU.mult)
                        S.op('pool', 'tensor_tensor', out=St[:].rearrange("p (r c) -> p r c", c=64),
                             in0=ropeT[:, 1, r0:r0 + 8].unsqueeze(2).to_broadcast([128, 8, 64]),
                             in1=ropeT[:, 3, 0:64].unsqueeze(1).to_broadcast([128, 8, 64]), op=ALU.add)
                        for j in range(4):
                            qk_chunk(hx_, n, j * 128, 0, qst[par][:, j, :], True)
                        S.dma('pool', out=qT_d[:, t0:t0 + 512].rearrange("(j p) t -> p j t", p=128), in_=qst[par][:])
                    qk_chunk(hx_, n, 512, 2, KT[:, key0:key0 + n], lat)
                    for j in range(6):
                        pp = psA[2 + j % 2]
                        for k in range(8):
                            S.op('pe', 'matmul', out=pp[:, 0:n], lhsT=win[:, k, 1280 + j * 128:1280 + (j + 1) * 128],
                                 rhs=hx_[:, k, 0:n], start=(k == 0), stop=(k == 7))
                        S.op('dve' if j % 2 == 0 else 'act', 'tensor_copy' if j % 2 == 0 else 'copy',
                             out=xst[par][:, j, 0:n], in_=pp[:, 0:n])
                    if lat:
                        S.dma('pool', out=xbc_d[:, t0:t0 + 512].rearrange("(j p) t -> p j t", p=128), in_=xst[par][:])
                    else:
                        S.dma('pool', out=xbcc_d[:, :].rearrange("(j p) t -> p j t", p=128), in_=xst[par][:, :, 0:n])
                    for s in range(ns):
                        pv = psA[0]
                        for k in range(8):
                            S.op('pe', 'matmul', out=pv[:, 0:128], lhsT=hx_[:, k, s * 128:(s + 1) * 128], rhs=win[:, k, 640:768],
                                 start=(k == 0), stop=(k == 7))
                        S.op('dve', 'tensor_copy', out=Vt[:, kt0 + s, :, 0:64],
                             in_=pv[:, 0:128].rearrange("p (a d) -> p a d", a=2))
                        pd = psA[1]
                        for k in range(8):
                            S.op('pe', 'matmul', out=pd[:, 0:16], lhsT=hx_[:, k, s * 128:(s + 1) * 128], rhs=win[:, k, 2048:2064],
                                 start=(k == 0), stop=(k == 7))
                        ch = (t0 // 128 + s) if lat else (NCH + s)
                        S.op('dve', 'tensor_tensor', out=dt_all[:, ch, :], in0=pd[:, 0:16], in1=dtb[:], op=ALU.add)
                        if lat:
                            pz = psA[2 + s % 2]
                            for k in range(8):
                                S.op('pe', 'matmul', out=pz[:], lhsT=hx_[:, k, s * 128:(s + 1) * 128], rhs=win[:, k, 768:1280],
                                     start=(k == 0), stop=(k == 7))
                            S.op('act', 'activation', out=zst[par][:, s, :], in_=pz[:], func=AF.Silu)
                    if lat:
                        S.dma('pool', out=zs_d[t0:t0 + 512, :].rearrange("(s p) c -> p s c", p=128), in_=zst[par][:])
                dtt = sb(es, "dtt", [128, NCH + 2, 16])
                S.op('act', 'activation', out=dtt[:], in_=dt_all[:], func=AF.Abs)
                S.op('act', 'activation', out=dtt[:], in_=dtt[:], func=AF.Exp, scale=-1.0)
                S.op('act', 'activation', out=dtt[:], in_=dtt[:], func=AF.Ln, bias=onec[:, 0:1], scale=1.0)
                S.op('dve', 'scalar_tensor_tensor', out=dt_all[:], in0=dt_all[:], scalar=0.0, in1=dtt[:], op0=ALU.max,
                     op1=ALU.add)
                S.barrier(mark, 'ph_A'); chk('A')

            with contextlib.ExitStack() as es:
                cw = sb(es, "cw", [128, 6, 3])
                cbias = sb(es, "cbias", [128, 6])
                for w_ in range(3):
                    S.dma('sp', out=cw[:, :, w_], in_=hy_conv_w[0, w_, :].rearrange("(j p) -> p j", p=128),
                          allow_slow_non_contiguous=True)
                S.dma('sp', out=cbias[:], in_=hy_conv_b[0, :].rearrange("(j p) -> p j", p=128), allow_slow_non_contiguous=True)
                dskb = sb(es, "dskb", [128, 512])
                S.op('dve', 'tensor_copy', out=dskb[:].rearrange("p (h d) -> p h d", h=8),
                     in_=dskr[:].unsqueeze(2).to_broadcast([128, 8, 64]))
                snr = sb(es, "snr", [128, 512])
                S.dma('sp', out=snr[:], in_=hy_ssm_norm[0, :].partition_broadcast(128))
                Xp = [sb(es, "Xp%d" % i, [128, 6, 514]) for i in range(2)]
                Xa = [sb(es, "Xa%d" % i, [128, 6, 512], BF16) for i in range(2)]
                CTm = [sb(es, "CTm%d" % i, [128, 2, 512], BF16) for i in range(2)]
                gmask = sb(es, "gmask", [128, 2])
                S.memset('pool', gmask[:], 0.0)
                S.memset('pool', gmask[0:64, 0:1], 1.0)
                S.memset('pool', gmask[64:128, 1:2], 1.0)
                cv = sb(es, "cv", [128, 512])
                psX = ps(es, "psX", [128, 8, 128], BF16)
                xtok = [sb(es, "xtok%d" % i_, [128, 640]) for i_ in range(2)]
                Btok = [sb(es, "Btok%d" % i_, [128, 128], BF16) for i_ in range(2)]
                dta = [sb(es, "dta%d" % i_, [128, 16]) for i_ in range(2)]
                psMisc = ps(es, "psMisc", [128, 512])
                dec = [sb(es, "dec%d" % i_, [128, 48]) for i_ in range(2)]
                xdt = [sb(es, "xdt%d" % i_, [128, 2, 512], BF16) for i_ in range(2)]
                xdd = [sb(es, "xdd%d" % i_, [128, 2, 512], BF16) for i_ in range(2)]
                psSt = ps(es, "psSt", [128, 2, 256])
                H = [sb(es, "H%d" % i, [128, 256]) for i in range(2)]
                Hb = [sb(es, "Hb%d" % i, [128, 256], BF16) for i in range(2)]
                gin_t = [sb(es, "gin%d" % i, [128, 256], BF16) for i in range(2)]
                rhsD = [sb(es, "rhsD%d" % i_, [128, 8, 128]) for i_ in range(2)]
                psDT = ps(es, "psDT", [128, 512])
                Ew = [sb(es, "Ew%d" % i_, [128, 1024], BF16) for i_ in range(2)]
                GM = [sb(es, "GM%d" % i_, [128, 2, 128], BF16) for i_ in range(2)]
                Wm = [[sb(es, "Wm%d_%d" % (c_, i), [128, 8, 128], BF16) for i in range(2)] for c_ in range(2)]
                psY = ps(es, "psY", [128, 512])
                psO = [ps(es, "psO%d" % i, [128, 512]) for i in range(2)]
                yb = [sb(es, "yb%d" % i_, [128, 512]) for i_ in range(2)]
                t1 = [sb(es, "t1%d" % i_, [128, 512]) for i_ in range(2)]
                zt = [sb(es, "zt%d" % i, [128, 4, 512]) for i in range(2)]
                gss = [sb(es, "gss%d" % i_, [128, 2]) for i_ in range(2)]
                junk2 = [sb(es, "junk2%d" % i_, [128, 256], BF16) for i_ in range(2)]
                yo = [sb(es, "yo%d" % i_, [128, 512], BF16) for i_ in range(2)]
                yst = [sb(es, "yst%d" % i, [128, 4, 512], BF16) for i in range(2)]
                for d_ in range(2):
                    S.memset('pool', H[d_][:], 0.0)
                    S.memset('pool', Hb[d_][:], 0.0)

                def prep_super(src_d, t0, n, tot, par):
                    X = Xp[par]
                    lo = max(t0 - 1, 0)
                    hi = min(t0 + n + 1, tot)
                    if t0 == 0:
                        S.memset('pool', X[:, :, 0:1], 0.0)
                    if t0 + n == tot:
                        S.memset('pool', X[:, :, n + 1:n + 2], 0.0)
                    S.dma('sp', out=X[:, :, 1 - (t0 - lo):1 + n + (hi - t0 - n)],
                          in_=src_d[:, lo:hi].rearrange("(j p) t -> p j t", p=128))
                    for j in range(6):
                        e1 = 'dve'
                        S.op(e1, 'tensor_scalar', out=cv[:, 0:n], in0=X[:, j, 1:n + 1], scalar1=cw[:, j, 1:2], scalar2=None,
                             op0=ALU.mult)
                        S.op(e1, 'scalar_tensor_tensor', out=cv[:, 0:n], in0=X[:, j, 0:n], scalar=cw[:, j, 0:1],
                             in1=cv[:, 0:n], op0=ALU.mult, op1=ALU.add)
                        S.op(e1, 'scalar_tensor_tensor', out=cv[:, 0:n], in0=X[:, j, 2:n + 2], scalar=cw[:, j, 2:3],
                             in1=cv[:, 0:n], op0=ALU.mult, op1=ALU.add)
                        S.op('act', 'activation', out=Xa[par][:, j, 0:n], in_=cv[:, 0:n], func=AF.Silu, bias=cbias[:, j:j + 1],
                             scale=1.0)

                def chunk_common(par, s, ch):
                    XA = Xa[par]
                    cb = ch % 2
                    for j in range(5):
                        S.op('pe', 'transpose', out=psX[:, j, :], in_=XA[:, j, s * 128:(s + 1) * 128], identity=ident[:])
                    S.op('act', 'copy', out=xtok[cb][:].rearrange("p (j c) -> p j c", j=5), in_=psX[:, 0:5, :])
                    S.op('pool', 'tensor_copy', out=Btok[cb][:], in_=xtok[cb][:, 512:640])
                    S.op('dve', 'tensor_tensor', out=dta[cb][:], in0=dt_all[:, ch, :], in1=arow[:], op=ALU.mult)
                    S.op('pe', 'matmul', out=psMisc[:, 256:264], lhsT=mk['le'][:], rhs=dta[cb][:, 0:8], start=True, stop=True)
                    S.op('pe', 'matmul', out=psMisc[:, 264:272], lhsT=mk['gt'][:], rhs=dta[cb][:, 0:8], start=True, stop=True)
                    S.op('pe', 'matmul', out=psMisc[:, 272:280], lhsT=mk['ge'][:], rhs=dta[cb][:, 8:16], start=True, stop=True)
                    S.op('pe', 'matmul', out=psMisc[:, 280:288], lhsT=mk['lt'][:], rhs=dta[cb][:, 8:16], start=True, stop=True)
                    S.op('pe', 'matmul', out=psMisc[:, 288:304], lhsT=ones_f[:], rhs=dta[cb][:], start=True, stop=True)
                    S.op('act', 'activation', out=dec[cb][:], in_=psMisc[:, 256:304], func=AF.Exp)

                def state_step(d_, ch, skip_xdt=False):
                    dcol = 8 if d_ == 0 else 24
                    cb = ch % 2
                    if not skip_xdt:
                        S.op('dve', 'tensor_tensor', out=xdt[cb][:, d_, :].rearrange("p (h d) -> p h d", h=8),
                             in0=xtok[cb][:, 0:512].rearrange("p (h d) -> p h d", h=8),
                             in1=dt_all[:, ch, d_ * 8:(d_ + 1) * 8].unsqueeze(2).to_broadcast([128, 8, 64]), op=ALU.mult)
                    S.op('pool', 'tensor_tensor', out=xdd[cb][:, d_, :].rearrange("p (h d) -> p h d", h=8),
                         in0=xdt[cb][:, d_, :].rearrange("p (h d) -> p h d", h=8),
                         in1=dec[cb][:, dcol:dcol + 8].unsqueeze(2).to_broadcast([128, 8, 64]), op=ALU.mult)
                    for g in range(2):
                        S.op('pe', 'matmul', out=psSt[:, g, :], lhsT=Btok[cb][:], rhs=xdd[cb][:, d_, g * 256:(g + 1) * 256], start=True,
                             stop=True)
                    for g in range(2):
                        hs = H[d_][g * 64:(g + 1) * 64, :]
                        S.op('dve', 'tensor_tensor', out=hs.rearrange("p (e d) -> p e d", e=4),
                             in0=hs.rearrange("p (e d) -> p e d", e=4),
                             in1=dec[cb][g * 64:(g + 1) * 64, 32 + d_ * 8 + g * 4:32 + d_ * 8 + g * 4 + 4].unsqueeze(2).to_broadcast(
                                 [64, 4, 64]), op=ALU.mult)
                        S.op('dve', 'tensor_tensor', out=hs, in0=hs, in1=psSt[g * 64:(g + 1) * 64, g, :], op=ALU.add)
                    S.op('pool', 'tensor_copy', out=Hb[d_][:], in_=H[d_][:])

                chkb('S0')
                prep_super(xbcc_d, 0, 256, 256, 0)
                chkb('Sp')
                for s in (0, 1):
                    chunk_common(0, s, NCH + s)
                    state_step(0, NCH + s)
                for s in (1, 0):
                    chunk_common(0, s, NCH + s)
                    state_step(1, NCH + s)
                chkb('Sc')
                def s1_prep(ch):
                    si, s = ch // 4, ch % 4
                    if s == 3:
                        prep_super(xbc_d, si * 512, 512, T, si % 2)
                    chunk_common(si % 2, s, ch)

                s1_prep(NCH - 1)
                for ch in range(NCH - 1, -1, -1):
                    if ch - 1 >= 0:
                        s1_prep(ch - 1)
                    gt_ = gin_t[ch % 2]
                    S.op('act', 'copy', out=gt_[:], in_=Hb[1][:])
                    S.dma('pool', out=gin_d.p(ch)[ch, :, :], in_=gt_[:])
                    state_step(1, ch)
                chkb('S1')
                def s2_stage1(ch):
                    si, s = ch // 4, ch % 4
                    par = si % 2
                    if s == 0:
                        prep_super(xbc_d, si * 512, 512, T, par)
                        for g in range(2):
                            S.op('act', 'activation', out=CTm[par][:, g, :], in_=Xa[par][:, 5, :], func=AF.Identity,
                                 scale=gmask[:, g:g + 1])
                        S.dma('sp', out=zt[par][:], in_=zs_d[si * 512:(si + 1) * 512, :].rearrange("(s p) c -> p s c", p=128))
                    cb = ch % 2
                    XA = Xa[par]
                    gt_ = gin_t[ch % 2]
                    S.dma('sp', out=gt_[:], in_=gin_d.p(ch)[ch, :, :])
                    chunk_common(par, s, ch)
                    for g in range(2):
                        S.op('pe', 'matmul', out=psMisc[:, g * 128:(g + 1) * 128], lhsT=XA[:, 4, s * 128:(s + 1) * 128],
                             rhs=CTm[par][:, g, s * 128:(s + 1) * 128], start=True, stop=True)
                    for d_ in range(2):
                        m1 = mk['le'] if d_ == 0 else mk['ge']
                        m2 = mk['gt'] if d_ == 0 else mk['lt']
                        S.op('pool', 'tensor_tensor', out=rhsD[d_][:],
                             in0=m1[:].unsqueeze(1).to_broadcast([128, 8, 128]),
                             in1=dta[cb][:, d_ * 8:(d_ + 1) * 8].unsqueeze(2).to_broadcast([128, 8, 128]), op=ALU.mult)
                        for hh in range(2):
                            S.op('pe', 'matmul', out=psDT[:], lhsT=m2[:],
                                 rhs=rhsD[d_][:, hh * 4:(hh + 1) * 4, :].rearrange("p h l -> p (h l)"), start=True, stop=True)
                            S.op('act', 'activation', out=Ew[d_][:, hh * 512:(hh + 1) * 512], in_=psDT[:], func=AF.Exp)
                        S.op('dve', 'tensor_tensor', out=GM[d_][:], in0=psMisc[:, 0:256].rearrange("p (g l) -> p g l", g=2), in1=m1[:].unsqueeze(1).to_broadcast([128, 2, 128]),
                             op=ALU.mult)
                        S.op('dve' if d_ == 0 else 'pool', 'tensor_tensor',
                             out=Wm[cb][d_][:].rearrange("p (g e) l -> p g e l", g=2),
                             in0=Ew[d_][:].rearrange("p (g e l) -> p g e l", g=2, e=4),
                             in1=GM[d_][:].unsqueeze(2).to_broadcast([128, 2, 4, 128]), op=ALU.mult)
                        S.op('dve', 'tensor_tensor', out=xdt[cb][:, d_, :].rearrange("p (h d) -> p h d", h=8),
                             in0=xtok[cb][:, 0:512].rearrange("p (h d) -> p h d", h=8),
                             in1=dt_all[:, ch, d_ * 8:(d_ + 1) * 8].unsqueeze(2).to_broadcast([128, 8, 64]), op=ALU.mult)

                def s2_stage2(ch):
                    si, s = ch // 4, ch % 4
                    par = si % 2
                    cb = ch % 2
                    XA = Xa[par]
                    gt_ = gin_t[ch % 2]
                    for h in range(8):
                        for d_ in range(2):
                            S.op('pe', 'matmul', out=psY[:, h * 64:(h + 1) * 64], lhsT=Wm[cb][d_][:, h, :],
                                 rhs=xdt[cb][:, d_, h * 64:(h + 1) * 64], start=(d_ == 0), stop=(d_ == 1))
                    for d_ in range(2):
                        st_ = Hb[0] if d_ == 0 else gt_
                        for g in range(2):
                            S.op('pe', 'matmul', out=psO[d_][:, g * 256:(g + 1) * 256],
                                 lhsT=CTm[par][:, g, s * 128:(s + 1) * 128], rhs=st_[:, :],
                                 start=True, stop=True)
                    S.op('pool', 'tensor_tensor', out=t1[cb][:], in0=xtok[cb][:, 0:512], in1=dskb[:], op=ALU.mult)
                    S.op('dve', 'tensor_tensor', out=yb[cb][:], in0=psY[:], in1=t1[cb][:], op=ALU.add)
                    for d_ in range(2):
                        ecol = 0 if d_ == 0 else 16
                        S.op('dve', 'tensor_tensor', out=t1[cb][:].rearrange("p (h d) -> p h d", h=8),
                             in0=psO[d_][:].rearrange("p (h d) -> p h d", h=8),
                             in1=dec[cb][:, ecol:ecol + 8].unsqueeze(2).to_broadcast([128, 8, 64]), op=ALU.mult)
                        S.op('dve', 'tensor_tensor', out=yb[cb][:], in0=yb[cb][:], in1=t1[cb][:], op=ALU.add)
                    state_step(0, ch, skip_xdt=True)
                    S.op('dve', 'tensor_tensor', out=yb[cb][:], in0=yb[cb][:], in1=zt[par][:, s, :], op=ALU.mult)
                    S.memset('pool', gss[cb][:], 0.0)
                    for g in range(2):
                        S.op('act', 'activation', out=junk2[cb][:], in_=yb[cb][:, g * 256:(g + 1) * 256], func=AF.Square,
                             accum_out=gss[cb][:, g:g + 1])
                    S.op('dve', 'tensor_scalar', out=gss[cb][:], in0=gss[cb][:], scalar1=1.0 / 256, scalar2=EPS, op0=ALU.mult,
                         op1=ALU.add)
                    S.op('act', 'activation', out=gss[cb][:], in_=gss[cb][:], func=AF.Sqrt)
                    S.op('dve', 'reciprocal', out=gss[cb][:], in_=gss[cb][:])
                    S.op('dve', 'tensor_tensor', out=yb[cb][:].rearrange("p (g c) -> p g c", g=2),
                         in0=yb[cb][:].rearrange("p (g c) -> p g c", g=2),
                         in1=gss[cb][:].unsqueeze(2).to_broadcast([128, 2, 256]), op=ALU.mult)
                    S.op('pool', 'tensor_tensor', out=yo[cb][:], in0=yb[cb][:], in1=snr[:], op=ALU.mult)
                    for j in range(4):
                        S.op('pe', 'transpose', out=psX[:, j, :], in_=yo[cb][:, j * 128:(j + 1) * 128], identity=ident[:])
                    S.op('act', 'copy', out=yst[par][:, :, s * 128:(s + 1) * 128], in_=psX[:, 0:4, :])
                    if s == 3:
                        S.dma('pool', out=ysT_d[:, si * 512:(si + 1) * 512].rearrange("(j p) t -> p j t", p=128), in_=yst[par][:])

                s2_stage1(0)
                for ch in range(NCH):
                    if ch + 1 < NCH:
                        s2_stage1(ch + 1)
                    s2_stage2(ch)

                S.barrier(mark, 'ph_S'); chk('S')

            with contextlib.ExitStack() as es:
                grow = sb(es, "grow", [128, D])
                S.dma('sp', out=grow[:], in_=grow_d[0, :].partition_broadcast(128))
                woA = sb(es, "woA", [64, 8, D], BF16)
                woB = sb(es, "woB", [128, 4, D], BF16)
                wst = [sb(es, "wst%d" % i, [128, D]) for i in range(2)]
                for h in range(8):
                    st = wst[h % 2]
                    S.dma('sp', out=st[0:64, :], in_=hy_w_out[0, h * 64:(h + 1) * 64, :])
                    S.op('dve', 'tensor_tensor', out=woA.p(h)[:, h, :], in0=st[0:64, :], in1=grow[0:64, :], op=ALU.mult)
                for j in range(4):
                    st = wst[j % 2]
                    S.dma('sp', out=st[:], in_=hy_w_out[0, 512 + j * 128:512 + (j + 1) * 128, :])
                    S.op('dve', 'tensor_tensor', out=woB.p(j)[:, j, :], in0=st[:], in1=grow[:], op=ALU.mult)
                qt = [sb(es, "qt%d" % i, [128, 4, 512], BF16) for i in range(2)]
                ysl = [sb(es, "ysl%d" % i, [128, 4, 512], BF16) for i in range(2)]
                xr = [sb(es, "xr%d" % i, [128, D]) for i in range(2)]
                PT = [sb(es, "PT%d" % i, [128, 1024], BF16) for i in range(3)]
                psSc = [ps(es, "psSc%d" % i, [128, 1024]) for i in range(3)]
                psOa = [ps(es, "psOa0", [65, 512])] * 2
                psMo = [ps(es, "psMo0", [128, 512])] * 2
                Osb = [sb(es, "Osb%d" % i, [128, 512]) for i in range(2)]
                for o_ in Osb:
                    S.memset('pool', o_[:], 0.0)
                rec = sb(es, "rec", [64, 512])
                mixT = [sb(es, "mixT%d" % i, [64, 8, 512], BF16) for i in range(2)]
                LOOK = 2
                assert NKT % 2 == 0
                NKP = NKT // 2
                iters = [(i, h, kp) for i in range(NT) for h in range(8) for kp in range(NKP)]
                pending = []

                def load_tile(i):
                    par = i % 2
                    t0 = i * 512
                    for kv in range(2):
                        S.dma('sp', out=qt[par][kv * 64:(kv + 1) * 64, :, :],
                              in_=qT_d[kv * 256:(kv + 1) * 256, t0:t0 + 512].rearrange("(e d) t -> d e t", d=64))
                    S.dma('sp', out=ysl[par][:], in_=ysT_d[:, t0:t0 + 512].rearrange("(j p) t -> p j t", p=128))

                def rec_score(n):
                    i, h, kp = iters[n]
                    if h == 0 and kp == 0:
                        load_tile(i)
                    kv, e = h // 4, h % 4
                    for u in range(2):
                        kt = 2 * kp + u
                        S.op('pe', 'matmul', out=psSc[n % 3][:, u * 512:(u + 1) * 512],
                             lhsT=KT[kv * 64:(kv + 1) * 64, kt * 128:(kt + 1) * 128],
                             rhs=qt[i % 2][kv * 64:(kv + 1) * 64, e, :], start=True, stop=True)

                def mk_final(i, h):
                    def f():
                        ob = Osb[h % 2]
                        S.op('pe', 'matmul', out=psMo[1][:, :], lhsT=bsel[:], rhs=ob[:], start=True, stop=True)
                        S.op('dve', 'reciprocal', out=rec[:], in_=psMo[1][0:64, :])
                        S.op('dve', 'tensor_tensor', out=mixT[i % 2][:, h, :], in0=ob[0:64, :], in1=rec[:], op=ALU.mult)
                    return f

                def mk_outproj(i, s, nh):
                    def f():
                        par = i % 2
                        t0 = i * 512
                        xr_ = xr[s % 2]
                        if nh == 0:
                            S.dma('sp', out=xr_[:], in_=x_in[t0 + s * 128:t0 + (s + 1) * 128, :])
                        pm = psMo[0]
                        for h in range(8):
                            S.op('pe', 'matmul', out=pm[:], lhsT=mixT[par][:, h, s * 128:(s + 1) * 128],
                                 rhs=woA[:, h, nh * 512:(nh + 1) * 512], start=(h == 0), stop=False)
                        for j in range(4):
                            S.op('pe', 'matmul', out=pm[:], lhsT=ysl[par][:, j, s * 128:(s + 1) * 128],
                                 rhs=woB[:, j, nh * 512:(nh + 1) * 512], start=False, stop=(j == 3))
                        S.op('dve', 'tensor_tensor', out=xr_[:, nh * 512:(nh + 1) * 512], in0=pm[:],
                             in1=xr_[:, nh * 512:(nh + 1) * 512], op=ALU.add)
                        if nh == 1:
                            S.dma('pool', out=x1_d[t0 + s * 128:t0 + (s + 1) * 128, :], in_=xr_[:])
                    return f

                for n in range(min(LOOK, len(iters))):
                    rec_score(n)
                since_final = 0
                for n, (i, h, kp) in enumerate(iters):
                    if n + LOOK < len(iters):
                        rec_score(n + LOOK)
                    kv = h // 4
                    po = psOa[h % 2]
                    S.op('act', 'activation', out=PT[n % 3][:], in_=psSc[n % 3][:], func=AF.Exp, scale=0.125)
                    for u in range(2):
                        kt = 2 * kp + u
                        S.op('pe', 'matmul', out=po[:], lhsT=Vt[:, kt, kv, :], rhs=PT[n % 3][:, u * 512:(u + 1) * 512],
                             start=(kt == 0), stop=(kt == NKT - 1))
                    if kp == NKP - 1:
                        S.op('dve', 'tensor_copy', out=Osb[h % 2][0:65, :], in_=po[:])
                        pending.append((n + 2, mk_final(i, h)))
                        if h == 7:
                            for s in range(4):
                                for nh in range(2):
                                    pending.append((n + 3, mk_outproj(i, s, nh)))
                    if pending and pending[0][0] <= n:
                        pending.pop(0)[1]()
                while pending:
                    pending.pop(0)[1]()
                S.barrier(mark, 'ph_B1'); chk('B1')

        def ffn_phase(l, src_d, dst_d, final):
            with contextlib.ExitStack() as es:
                wg = sb(es, "wg", [128, 8, FH], BF16)
                wu = sb(es, "wu", [128, 8, FH], BF16)
                wd = sb(es, "wd", [128, NJ, D], BF16)
                with contextlib.ExitStack() as es2:
                    grow = sb(es2, "growf", [128, D])
                    wst = [sb(es2, "wstf%d" % i, [128, D]) for i in range(2)]
                    S.dma('sp', out=grow[:], in_=grow_d[l * 2 + 1, :].partition_broadcast(128))
                    for k in range(8):
                        S.dma('pool', out=wg.p(k)[:, k, :], in_=w_gate[l, k * 128:(k + 1) * 128, :])
                        S.dma('pool', out=wu.p(k)[:, k, :], in_=w_up[l, k * 128:(k + 1) * 128, :])
                    for j in range(NJ):
                        st = wst[j % 2]
                        S.dma('sp', out=st[:], in_=w_down[l, j * 128:(j + 1) * 128, :])
                        S.op('dve' if j % 2 == 0 else 'pool', 'tensor_tensor', out=wd.p(j)[:, j, :], in0=st[:], in1=grow[:],
                             op=ALU.mult)
                    S.barrier(mark, 'ph_FW'); chk('FW')
                feb = fe_bufs(es, nhx=1)
                Acol = cols[:, l * 32 + 16:l * 32 + 24]
                Bcol = cols[:, l * 32 + 24:l * 32 + 32]
                aT = sb(es, "aT", [128, NJ, 512], BF16)
                sg = [sb(es, "sg%d" % i, [128, 512]) for i in range(2)]
                xr = [sb(es, "xrf%d" % i, [128, D]) for i in range(2)]
                psGa = [ps(es, "psGa%d" % i, [128, 512]) for i in range(2)]
                psUa = [ps(es, "psUa%d" % i, [128, 512]) for i in range(2)]
                psD = [ps(es, "psD%d" % i, [128, 512]) for i in range(2)]
                if final:
                    fnr = sb(es, "fnr", [128, D])
                    S.dma('sp', out=fnr[:], in_=final_norm[:].partition_broadcast(128))
                    fss = sb(es, "fss", [128, 1])
                    fjunk = sb(es, "fjunk", [128, D], BF16)
                fe_load(feb, src_d, 0, 4)
                hx_ = fe_trans(feb, 4, Acol, Bcol, 0)
                for i in range(NT):
                    t0 = i * 512
                    for j in range(NJ):
                        pg, pu = psGa[j % 2], psUa[j % 2]
                        for k in range(8):
                            S.op('pe', 'matmul', out=pg[:], lhsT=wg[:, k, j * 128:(j + 1) * 128], rhs=hx_[:, k, :],
                                 start=(k == 0), stop=(k == 7))
                        for k in range(8):
                            S.op('pe', 'matmul', out=pu[:], lhsT=wu[:, k, j * 128:(j + 1) * 128], rhs=hx_[:, k, :],
                                 start=(k == 0), stop=(k == 7))
                        S.op('act', 'activation', out=sg[j % 2][:], in_=pg[:], func=AF.Silu)
                        S.op('dve', 'tensor_tensor', out=aT[:, j, :], in0=sg[j % 2][:], in1=pu[:], op=ALU.mult)
                        if i + 1 < NT and 1 <= j <= 4:
                            fe_load_sub(feb, src_d, t0 + 512, j - 1)
                    if i + 1 < NT:
                        hx_ = fe_trans(feb, 4, Acol, Bcol, 0)
                    for s in range(4):
                        xr_ = xr[s % 2]
                        S.dma('pool', out=xr_[:], in_=src_d[t0 + s * 128:t0 + (s + 1) * 128, :])
                        for nh in range(2):
                            pd_ = psD[nh]
                            for j in range(NJ):
                                S.op('pe', 'matmul', out=pd_[:], lhsT=aT[:, j, s * 128:(s + 1) * 128],
                                     rhs=wd[:, j, nh * 512:(nh + 1) * 512], start=(j == 0), stop=(j == NJ - 1))
                            S.op('dve', 'tensor_tensor', out=xr_[:, nh * 512:(nh + 1) * 512], in0=pd_[:],
                                 in1=xr_[:, nh * 512:(nh + 1) * 512], op=ALU.add)
                        if final:
                            S.memset('pool', fss[:], 0.0)
                            S.op('act', 'activation', out=fjunk[:], in_=xr_[:], func=AF.Square, accum_out=fss[:, 0:1])
                            S.op('dve', 'tensor_scalar', out=fss[:], in0=fss[:], scalar1=1.0 / D, scalar2=EPS, op0=ALU.mult,
                                 op1=ALU.add)
                            S.op('act', 'activation', out=fss[:], in_=fss[:], func=AF.Sqrt)
                            S.op('dve', 'reciprocal', out=fss[:], in_=fss[:])
                            S.op('dve', 'scalar_tensor_tensor', out=xr_[:], in0=xr_[:], scalar=fss[:, 0:1],
                                 in1=fnr[:], op0=ALU.mult, op1=ALU.mult)
                        S.dma('pool', out=dst_d[t0 + s * 128:t0 + (s + 1) * 128, :], in_=xr_[:])
                S.barrier(mark, 'ph_F'); chk('F')

        ffn_phase(0, x1_d, x2_d, False)

        with contextlib.ExitStack() as es:
            wi = sb(es, "wi", [128, 8, 3 * D], BF16)
            wo = sb(es, "wo", [128, 8, D], BF16)
            scw = sb(es, "scw", [128, 8, 3])
            with contextlib.ExitStack() as es2:
                grow = sb(es2, "growc", [128, D])
                wst = [sb(es2, "wstc%d" % i, [128, D]) for i in range(2)]
                S.dma('sp', out=grow[:], in_=grow_d[2, :].partition_broadcast(128))
                for w_ in range(3):
                    S.dma('sp', out=scw[:, :, w_], in_=sc_conv_w[0, w_, :].rearrange("(j p) -> p j", p=128),
                          allow_slow_non_contiguous=True)
                for k in range(8):
                    S.dma('pool', out=wi.p(k)[:, k, :], in_=sc_w_in[0, k * 128:(k + 1) * 128, :])
                    st = wst[k % 2]
                    S.dma('sp', out=st[:], in_=sc_w_out[0, k * 128:(k + 1) * 128, :])
                    S.op('dve', 'tensor_tensor', out=wo.p(k)[:, k, :], in0=st[:], in1=grow[:], op=ALU.mult)
                S.barrier(mark, 'ph_CW'); chk('CW')
            feb = fe_bufs(es, npt=1)
            gbt = [sb(es, "gbt%d" % i, [128, 8, 512]) for i in range(2)]
            vt = [sb(es, "vt%d" % i, [128, 8, 514]) for i in range(2)]
            gcs = [sb(es, "gcs%d" % i, [128, 512]) for i in range(2)]
            cvc = [sb(es, "cvc%d" % i, [128, 512]) for i in range(2)]
            wT = sb(es, "wT", [128, 8, 512], BF16)
            xr = [sb(es, "xrc%d" % i, [128, D]) for i in range(2)]
            psC = [[ps(es, "psC%d_%d" % (a_, i), [128, 512]) for i in range(3)] for a_ in range(2)]
            psMc = ps(es, "psMc", [128, 512])
            Acol = cols[:, 32:40]
            Bcol = cols[:, 40:48]

            def conv_chunk(pq, j):
                Vv = vt[pq]
                cv_ = cvc[j % 2]
                S.op('dve', 'tensor_scalar', out=cv_[:], in0=Vv[:, j, 1:513], scalar1=scw[:, j, 1:2], scalar2=None,
                     op0=ALU.mult)
                S.op('dve', 'scalar_tensor_tensor', out=cv_[:], in0=Vv[:, j, 0:512], scalar=scw[:, j, 0:1], in1=cv_[:],
                     op0=ALU.mult, op1=ALU.add)
                S.op('dve', 'scalar_tensor_tensor', out=cv_[:], in0=Vv[:, j, 2:514], scalar=scw[:, j, 2:3], in1=cv_[:],
                     op0=ALU.mult, op1=ALU.add)
                S.op('pool', 'tensor_tensor', out=wT[:, j, :], in0=cv_[:], in1=gbt[pq][:, j, :], op=ALU.mult)

            def out_proj(tp):
                for s in range(4):
                    xr_ = xr[s % 2]
                    S.dma('sp', out=xr_[:], in_=x2_d[tp + s * 128:tp + (s + 1) * 128, :])
                    for nh in range(2):
                        pm = psMc
                        for k in range(8):
                            S.op('pe', 'matmul', out=pm[:], lhsT=wT[:, k, s * 128:(s + 1) * 128],
                                 rhs=wo[:, k, nh * 512:(nh + 1) * 512], start=(k == 0), stop=(k == 7))
                        S.op('dve', 'tensor_tensor', out=xr_[:, nh * 512:(nh + 1) * 512], in0=pm[:],
                             in1=xr_[:, nh * 512:(nh + 1) * 512], op=ALU.add)
                    S.dma('pool', out=x3_d[tp + s * 128:tp + (s + 1) * 128, :], in_=xr_[:])

            fe_load(feb, x2_d, 0, 4)
            hx_ = fe_trans(feb, 4, Acol, Bcol, 0)
            for i in range(NT):
                par = i % 2
                if i + 1 < NT:
                    fe_load(feb, x2_d, (i + 1) * 512, 4)
                for j in range(8):
                    pb_, pc_, pu_ = psC[j % 2]
                    for (pp_, cbase) in ((pc_, D), (pu_, 2 * D), (pb_, 0)):
                        for k in range(8):
                            S.op('pe', 'matmul', out=pp_[:], lhsT=wi[:, k, cbase + j * 128:cbase + (j + 1) * 128],
                                 rhs=hx_[:, k, :], start=(k == 0), stop=(k == 7))
                    S.op('act', 'copy', out=gcs[j % 2][:], in_=pc_[:])
                    S.op('dve', 'tensor_tensor', out=vt[par][:, j, 1:513], in0=gcs[j % 2][:], in1=pu_[:], op=ALU.mult)
                    S.op('act', 'copy', out=gbt[par][:, j, :], in_=pb_[:])
                    if i == 0:
                        S.memset('pool', vt[par][:, j, 0:1], 0.0)
                    else:
                        S.op('pool', 'tensor_copy', out=vt[par][:, j, 0:1], in_=vt[1 - par][:, j, 512:513])
                        S.op('pool', 'tensor_copy', out=vt[1 - par][:, j, 513:514], in_=vt[par][:, j, 1:2])
                        conv_chunk(1 - par, j)
                if i + 1 < NT:
                    hx_ = fe_trans(feb, 4, Acol, Bcol, i + 1)
                if i >= 1:
                    out_proj((i - 1) * 512)
            lp = (NT - 1) % 2
            for j in range(8):
                S.memset('pool', vt[lp][:, j, 513:514], 0.0)
                conv_chunk(lp, j)
            out_proj((NT - 1) * 512)
            S.barrier(mark, 'ph_C1'); chk('C1')

        ffn_phase(1, x3_d, out, True)
        S.flush()
    except _Stop:
        pass
    return nc, S


_CACHE = {}


def _consts():
    p = np.arange(128)
    i = (p % 64) // 2
    inv = 1.0 / (10000.0 ** ((2.0 * (i % 16)) / 32.0))
    fr = np.where(i < 16, inv, 0.0)
    fc = np.where(i >= 16, inv, 0.0)
    sign = np.where(p % 2 == 0, -1.0, 1.0)
    kc = np.stack([fr, fc, sign, np.zeros(128)], axis=1).astype(np.float32)
    return kc, np.arange(128, dtype=np.float32)


def kernel(**inputs):
    x = np.ascontiguousarray(inputs['x'], dtype=np.float32)
    B, T, _ = x.shape
    import os
    stop = os.environ.get("K_STOP")
    if (T, stop) not in _CACHE:
        _CACHE[(T, stop)] = build(T, stop)
    nc, _ = _CACHE[(T, stop)]
    kc, pos = _consts()
    shared = {}
    for k in ('ada_w', 'ada_b', 'norm_mix', 'norm_ffn', 'ffn_w_gate', 'ffn_w_up', 'ffn_w_down', 'hy_w_in', 'hy_q_norm',
              'hy_k_norm', 'hy_conv_w', 'hy_conv_b', 'hy_d_skip', 'hy_ssm_norm', 'hy_w_out', 'sc_w_in', 'sc_conv_w',
              'sc_w_out', 'final_norm', 'c_ctx'):
        shared[k] = np.ascontiguousarray(inputs[k], dtype=np.float32)
    shared['hy_dt_bias'] = np.ascontiguousarray(inputs['hy_dt_bias'], dtype=np.float32).reshape(1, 16)
    shared['hy_a_log'] = np.ascontiguousarray(inputs['hy_a_log'], dtype=np.float32).reshape(1, 16)
    shared['kconst'] = kc
    shared['kpos'] = pos
    in_maps = []
    for b in range(B):
        m = dict(shared)
        m['x'] = x[b]
        m['c'] = np.ascontiguousarray(inputs['c'][b], dtype=np.float32)
        m['ctx'] = np.ascontiguousarray(inputs['ctx'][b], dtype=np.float32)
        in_maps.append(m)
    res = run_bass_kernel_spmd(nc, in_maps, core_ids=list(range(B)))
    return np.stack([res.results[b]['out'] for b in range(B)], axis=0).astype(np.float32)
```

```python
import contextlib
import math
import numpy as np
import concourse.bass as bass
import concourse.mybir as mybir
from concourse.bass_utils import run_bass_kernel_spmd

F32 = mybir.dt.float32
BF16 = mybir.dt.bfloat16
AF = mybir.ActivationFunctionType
ALU = mybir.AluOpType
EPS = 1e-6
D = 1024
KD = 8
CTX = 256
FH = 2816
NJ = 22
INP = 2064
ENGS = ('pe', 'act', 'dve', 'pool', 'sp')


class Tr:
    __slots__ = ('w', 'r')

    def __init__(self):
        self.w = None
        self.r = {}


class V:
    def __init__(self, buf, ap, part=None):
        self.buf = buf
        self.ap = ap
        self.part = part

    def __getitem__(self, k):
        return V(self.buf, self.ap[k], self.part)

    def rearrange(self, *a, **k):
        return V(self.buf, self.ap.rearrange(*a, **k), self.part)

    def to_broadcast(self, shape):
        return V(self.buf, self.ap.to_broadcast(shape), self.part)

    def unsqueeze(self, ax):
        return V(self.buf, self.ap.unsqueeze(ax), self.part)

    def partition_broadcast(self, n):
        return V(self.buf, self.ap.partition_broadcast(n), self.part)


class PV:
    def __init__(self, buf, key):
        self.buf = buf
        self.key = key

    def __getitem__(self, k):
        return V(self.buf, self.buf.t[k], self.key)


class Buf:
    def __init__(self, t, track=True):
        self.t = t
        self.track = track
        self.base = Tr()
        self.parts = {}

    def __getitem__(self, k):
        return V(self, self.t[k], None)

    def p(self, key):
        return PV(self, key)


class Sched:
    def __init__(self, nc, nds=32):
        self.nc = nc
        self.e = {'pe': nc.tensor, 'act': nc.scalar, 'dve': nc.vector, 'pool': nc.gpsimd, 'sp': nc.sync}
        self.sem = {k: nc.alloc_semaphore('sm_' + k) for k in ENGS}
        self.dsem = {}
        for q in ('sp', 'pool'):
            for i in range(nds):
                self.dsem[(q, i)] = nc.alloc_semaphore('sd_%s_%d' % (q, i))
        self.nds = nds
        self.cnt = {k: 0 for k in ENGS}
        self.dval = {k: 0 for k in self.dsem}
        self.ndma = {'sp': 0, 'pool': 0}
        self.waited = {k: {} for k in ENGS}
        self.ops = []
        self.n_ins = 0

    def _trs(self, v, write):
        b = v.buf
        if not b.track:
            return [], None
        if v.part is None:
            chk = [b.base] + list(b.parts.values())
            rec = b.base
        else:
            if v.part not in b.parts:
                b.parts[v.part] = Tr()
            chk = [b.base, b.parts[v.part]]
            rec = b.parts[v.part]
        return chk, rec

    def _record(self, eng, isdma, meth, kw):
        oid = len(self.ops)
        raw, oth = set(), set()
        recs_w, recs_r = [], []
        akw = {}
        for k, a in kw.items():
            if isinstance(a, V):
                akw[k] = a.ap
                w = k in ('out', 'accum_out', 'ap')
                chk, rec = self._trs(a, w)
                if rec is None:
                    continue
                for tr in chk:
                    if tr.w is not None:
                        (oth if w else raw).add(tr.w)
                    if w:
                        oth.update(tr.r.values())
                (recs_w if w else recs_r).append((a, rec))
            else:
                akw[k] = a
        deps = set(raw)
        for d in oth:
            o = self.ops[d]
            if (not isdma) and (not o['dma']) and o['eng'] == eng and eng == 'pe':
                continue
            deps.add(d)
        for d in deps:
            self.ops[d]['sig'] = True
        self.ops.append(dict(eng=eng, dma=isdma, meth=meth, kw=akw, deps=deps, sig=isdma, ev=None))
        for a, rec in recs_w:
            rec.w = oid
            rec.r = {}
            if a.part is None:
                for tr in a.buf.parts.values():
                    tr.w = None
                    tr.r = {}
        for a, rec in recs_r:
            rec.r[('d', oid) if isdma else eng] = oid
        return oid

    def op(self, eng, meth, **kw):
        return self._record(eng, False, meth, kw)

    def dma(self, q, **kw):
        return self._record(q, True, 'dma_start', kw)

    def _wait(self, eng, key, val):
        if self.waited[eng].get(key, 0) >= val:
            return
        sem = self.sem[key] if isinstance(key, str) else self.dsem[key]
        self.e[eng].wait_ge(sem, val)
        self.waited[eng][key] = val

    def flush(self):
        for o in self.ops:
            if o['ev'] is not None:
                continue
            eng = o['eng']
            for d in sorted(o['deps']):
                ev = self.ops[d]['ev']
                if ev is not None:
                    self._wait(eng, ev[0], ev[1])
            if o['dma']:
                i = (eng, self.ndma[eng] % self.nds)
                self.ndma[eng] += 1
                if self.dval[i] > 0:
                    self._wait(eng, i, self.dval[i])
                ins = self.e[eng].dma_start(**o['kw'])
                self.dval[i] += 16
                ins.then_inc(self.dsem[i], 16)
                o['ev'] = (i, self.dval[i])
            else:
                ins = getattr(self.e[eng], o['meth'])(**o['kw'])
                if o['sig']:
                    self.cnt[eng] += 1
                    ins.then_inc(self.sem[eng], 1)
                    o['ev'] = (eng, self.cnt[eng])
                else:
                    o['ev'] = ('none', 0)
            self.n_ins += 1
            o['kw'] = None

    def memset(self, eng, v, val):
        return self._record(eng, False, 'memset', {'ap': v, 'constant': val})

    def barrier(self, tile=None, name=None):
        if name is not None:
            with self.nc.named_scope(name):
                return self.barrier(tile, None)
        last = {}
        for o in self.ops:
            if o['ev'] is None and not o['dma']:
                last[o['eng']] = o
        for o in last.values():
            o['sig'] = True
        self.flush()
        for eng in ENGS:
            for k in ('pe', 'act', 'dve', 'pool'):
                self._wait(eng, k, self.cnt[k])
            for i in self.dsem:
                if self.dval[i] > 0:
                    self._wait(eng, i, self.dval[i])
        self.ops = []
        self._reset_trackers()

    def _reset_trackers(self):
        for b in self.allbufs:
            b.base = Tr()
            b.parts = {}

    allbufs = []


class _Stop(Exception):
    pass


def build(T, stop=None):
    NT = T // 512
    NCH = T // 128
    SK = T + CTX
    NKT = SK // 128
    nc = bass.Bass("TRN2", target_bir_lowering=False)
    S = Sched(nc)
    S.allbufs = []
    top = contextlib.ExitStack()

    def dram(name, shape, dt=F32, kind="ExternalInput"):
        return Buf(nc.dram_tensor(name, list(shape), dt, kind=kind).ap(), track=False)

    x_in = dram("x", [T, D])
    c_in = dram("c", [D])
    ctx_in = dram("ctx", [CTX, D])
    cctx_in = dram("c_ctx", [D])
    ada_w = dram("ada_w", [2, D, 6 * D])
    ada_b = dram("ada_b", [2, 6 * D])
    norm_mix = dram("norm_mix", [2, D])
    norm_ffn = dram("norm_ffn", [2, D])
    w_gate = dram("ffn_w_gate", [2, D, FH])
    w_up = dram("ffn_w_up", [2, D, FH])
    w_down = dram("ffn_w_down", [2, FH, D])
    hy_w_in = dram("hy_w_in", [1, D, INP])
    hy_q_norm = dram("hy_q_norm", [1, 64])
    hy_k_norm = dram("hy_k_norm", [1, 64])
    hy_conv_w = dram("hy_conv_w", [1, 3, 768])
    hy_conv_b = dram("hy_conv_b", [1, 768])
    hy_dt_bias = dram("hy_dt_bias", [1, 16])
    hy_a_log = dram("hy_a_log", [1, 16])
    hy_d_skip = dram("hy_d_skip", [1, 8])
    hy_ssm_norm = dram("hy_ssm_norm", [1, 512])
    hy_w_out = dram("hy_w_out", [1, D, D])
    sc_w_in = dram("sc_w_in", [1, D, 3 * D])
    sc_conv_w = dram("sc_conv_w", [1, 3, D])
    sc_w_out = dram("sc_w_out", [1, D, D])
    final_norm = dram("final_norm", [D])
    kc_in = dram("kconst", [128, 4])
    pos_in = dram("kpos", [128])
    out = dram("out", [T, D], kind="ExternalOutput")
    qT_d = dram("s_qT", [512, T], BF16, "Internal")
    zs_d = dram("s_zs", [T, 512], F32, "Internal")
    xbc_d = dram("s_xbc", [768, T], F32, "Internal")
    xbcc_d = dram("s_xbcc", [768, CTX], F32, "Internal")
    ysT_d = dram("s_ysT", [512, T], BF16, "Internal")
    gin_d = dram("s_gin", [NCH, 128, 256], BF16, "Internal")
    gin_d.track = True
    S.allbufs.append(gin_d)
    grow_d = dram("s_grow", [4, D], F32, "Internal")
    x1_d = dram("s_x1", [T, D], F32, "Internal")
    x2_d = dram("s_x2", [T, D], F32, "Internal")
    x3_d = dram("s_x3", [T, D], F32, "Internal")

    uid = [0]

    def sb(es, name, shape, dt=F32):
        uid[0] += 1
        b = Buf(es.enter_context(nc.sbuf_tensor("%s_%d" % (name, uid[0]), list(shape), dt)))
        S.allbufs.append(b)
        return b

    def ps(es, name, shape, dt=F32):
        uid[0] += 1
        b = Buf(es.enter_context(nc.psum_tensor("%s_%d" % (name, uid[0]), list(shape), dt)))
        S.allbufs.append(b)
        return b

    def chk(nm):
        if stop == nm:
            raise _Stop()

    def chkb(nm):
        if stop == nm:
            S.barrier(None)
            raise _Stop()

    try:
      with top:
        ident_f = sb(top, "ident_f", [128, 128])
        ident = sb(top, "ident", [128, 128], BF16)
        ones_f = sb(top, "ones_f", [128, 128])
        blk = sb(top, "blk", [128, 128], BF16)
        bsel = sb(top, "bsel", [128, 128])
        cols = sb(top, "cols", [128, 64])
        gcol = sb(top, "gcol", [128, 4])
        onec = sb(top, "onec", [128, 1])
        arow = sb(top, "arow", [128, 16])
        dskr = sb(top, "dskr", [128, 8])
        mark = {'act': sb(top, "mk_a", [1, 4]), 'dve': sb(top, "mk_d", [1, 4]), 'pool': sb(top, "mk_p", [1, 4]),
                'pe_sb': sb(top, "mk_pe", [1, 4], BF16)}
        colsc = sb(top, "colsc", [128, 16])
        mid = contextlib.ExitStack()
        mk = {n: sb(mid, "mk_" + n, [128, 128]) for n in ('le', 'gt', 'ge', 'lt')}
        ropeT = sb(mid, "ropeT", [128, 4, 128])
        dt_all = sb(mid, "dt_all", [128, NCH + 2, 16])
        KT = sb(mid, "KT", [128, SK], BF16)
        Vt = sb(mid, "Vt", [128, NKT, 2, 65], BF16)

        S.memset('pool', ident_f[:], 0.0)
        S.op('pool', 'affine_select', out=ident_f[:], in_=ident_f[:], pattern=[[-1, 128]], compare_op=ALU.not_equal,
             fill=1.0, base=0, channel_multiplier=1)
        S.op('dve', 'tensor_copy', out=ident[:], in_=ident_f[:])
        S.memset('pool', ones_f[:], 1.0)
        for n, cmp_, sg_ in (('le', ALU.is_ge, -1), ('gt', ALU.is_gt, 1), ('ge', ALU.is_ge, 1), ('lt', ALU.is_gt, -1)):
            S.op('pool', 'affine_select', out=mk[n][:], in_=ones_f[:], pattern=[[-sg_, 128]], compare_op=cmp_,
                 fill=0.0, base=0, channel_multiplier=sg_)
        S.memset('pool', blk[:], 0.0)
        S.memset('pool', blk[0:64, 0:64], 1.0 / 64)
        S.memset('pool', blk[64:128, 64:128], 1.0 / 64)
        S.memset('pool', bsel[:], 0.0)
        S.memset('pool', bsel[64:65, 0:64], 1.0)
        S.memset('pool', mark['pe_sb'][:], 0.0)
        S.memset('pool', onec[:], 1.0)
        for e_ in ('act', 'dve', 'pool'):
            S.memset('pool', mark[e_][:], 0.0)

        with contextlib.ExitStack() as es:
            ccol = sb(es, "ccol", [128, 2, 8])
            scb = sb(es, "scb", [128, 2, 8, 128])
            awt = [sb(es, "awt%d" % i, [128, 8, 1024]) for i in range(2)]
            brow = [sb(es, "brow%d" % i, [128, 1024]) for i in range(2)]
            nrow = sb(es, "nrow", [128, 4, 8])
            Rt = sb(es, "Rt", [128, 1024])
            tmpd = sb(es, "tmpd", [128, 1024])
            psR = [ps(es, "psR%d" % i, [128, 512]) for i in range(2)]
            S.dma('sp', out=ccol[:, 0, :], in_=c_in[:].rearrange("(k p) -> p k", p=128), allow_slow_non_contiguous=True)
            S.dma('sp', out=ccol[:, 1, :], in_=cctx_in[:].rearrange("(k p) -> p k", p=128), allow_slow_non_contiguous=True)
            for l in range(2):
                S.dma('sp', out=nrow[:, 2 * l, :], in_=norm_mix[l, :].rearrange("(k p) -> p k", p=128),
                      allow_slow_non_contiguous=True)
                S.dma('sp', out=nrow[:, 2 * l + 1, :], in_=norm_ffn[l, :].rearrange("(k p) -> p k", p=128),
                      allow_slow_non_contiguous=True)
            S.op('act', 'activation', out=ccol[:], in_=ccol[:], func=AF.Silu)
            for v_ in range(2):
                for k in range(8):
                    S.op('dve', 'tensor_copy', out=scb[:, v_, k, :], in_=ccol[:, v_, k:k + 1].to_broadcast([128, 128]))
            gi = 0
            jobs = [(0, 0, g) for g in range(6)] + [(0, 1, 0), (0, 1, 1)] + [(1, 0, g) for g in range(6)]
            for (l, v_, g) in jobs:
                a = awt[gi % 2]
                br = brow[gi % 2]
                gi += 1
                for k in range(8):
                    S.dma('sp', out=a.p(k)[:, k, :], in_=ada_w[l, k * 128:(k + 1) * 128, g * 1024:(g + 1) * 1024])
                S.dma('sp', out=br[:], in_=ada_b[l, g * 1024:(g + 1) * 1024].partition_broadcast(128))
                for h in range(2):
                    for k in range(8):
                        S.op('pe', 'matmul', out=psR[h][:], lhsT=scb[:, v_, k, :], rhs=a.p(k)[:, k, h * 512:(h + 1) * 512],
                             start=(k == 0), stop=(k == 7))
                    S.op('dve', 'tensor_tensor', out=Rt[:, h * 512:(h + 1) * 512], in0=psR[h][:],
                         in1=br[:, h * 512:(h + 1) * 512], op=ALU.add)
                if g in (2, 5) and v_ == 0:
                    S.dma('sp', out=grow_d[l * 2 + (0 if g == 2 else 1), :], in_=Rt[0:1, :])
                else:
                    S.op('dve', 'tensor_tensor', out=tmpd[:].rearrange("p (k f) -> p k f", k=8),
                         in0=Rt[:].rearrange("p (k f) -> p k f", k=8),
                         in1=ident_f[:].unsqueeze(1).to_broadcast([128, 8, 128]), op=ALU.mult)
                    if v_ == 1:
                        dst = colsc[:, g * 8:(g + 1) * 8]
                    else:
                        off = {0: 8, 1: 0, 3: 24, 4: 16}[g]
                        dst = cols[:, l * 32 + off: l * 32 + off + 8]
                    S.op('dve', 'tensor_reduce', out=dst, in_=tmpd[:].rearrange("p (k f) -> p k f", k=8),
                         axis=mybir.AxisListType.X, op=ALU.add)
                    if g in (1, 4):
                        nr = nrow[:, 2 * l + (0 if g == 1 else 1), :]
                        S.op('dve', 'scalar_tensor_tensor', out=dst, in0=dst, scalar=1.0, in1=nr, op0=ALU.add, op1=ALU.mult)
            for j, src in ((0, hy_q_norm), (2, hy_k_norm)):
                for half in range(2):
                    S.dma('sp', out=gcol[half * 64:(half + 1) * 64, j:j + 1], in_=src[0, :].rearrange("(d o) -> d o", o=1),
                          allow_slow_non_contiguous=True)
                    pr = src[0, :].rearrange("(i two) -> i two", two=2)
                    S.dma('sp', out=gcol[half * 64:(half + 1) * 64:2, j + 1:j + 2], in_=pr[:, 1:2],
                          allow_slow_non_contiguous=True)
                    S.dma('sp', out=gcol[half * 64 + 1:(half + 1) * 64:2, j + 1:j + 2], in_=pr[:, 0:1],
                          allow_slow_non_contiguous=True)
            kc = sb(es, "kc", [128, 4])
            posb = sb(es, "posb", [128, 128])
            ang = sb(es, "ang", [128, 128])
            S.dma('sp', out=kc[:], in_=kc_in[:])
            S.dma('sp', out=posb[:], in_=pos_in[:].partition_broadcast(128))
            angi = sb(es, "angi", [128, 128], mybir.dt.int32)
            angk = sb(es, "angk", [128, 128])
            angm = sb(es, "angm", [128, 128])
            for ti, (fcol, sh) in enumerate(((0, 0.5 * math.pi), (0, 0.0), (1, 0.5 * math.pi), (1, 0.0))):
                S.op('dve', 'tensor_scalar', out=ang[:], in0=posb[:], scalar1=kc[:, fcol:fcol + 1], scalar2=sh,
                     op0=ALU.mult, op1=ALU.add)
                S.op('dve', 'tensor_scalar', out=angk[:], in0=ang[:], scalar1=1.0 / (2 * math.pi), scalar2=None, op0=ALU.mult)
                S.op('dve', 'tensor_copy', out=angi[:], in_=angk[:])
                S.op('dve', 'tensor_copy', out=angk[:], in_=angi[:])
                S.op('dve', 'scalar_tensor_tensor', out=ang[:], in0=angk[:], scalar=-2 * math.pi, in1=ang[:], op0=ALU.mult,
                     op1=ALU.add)
                S.op('dve', 'tensor_scalar', out=angm[:], in0=ang[:], scalar1=math.pi, scalar2=-2 * math.pi, op0=ALU.is_gt,
                     op1=ALU.mult)
                S.op('dve', 'tensor_tensor', out=ang[:], in0=ang[:], in1=angm[:], op=ALU.add)
                S.op('dve', 'tensor_scalar', out=angm[:], in0=ang[:], scalar1=-1.0, scalar2=math.pi, op0=ALU.mult,
                     op1=ALU.is_gt)
                S.op('dve', 'scalar_tensor_tensor', out=ang[:], in0=angm[:], scalar=2 * math.pi, in1=ang[:], op0=ALU.mult,
                     op1=ALU.add)
                S.op('act', 'activation', out=ropeT[:, ti, :], in_=ang[:], func=AF.Sin)
            for ti in (1, 3):
                S.op('dve', 'tensor_scalar', out=ropeT[:, ti, :], in0=ropeT[:, ti, :], scalar1=kc[:, 2:3], scalar2=None,
                     op0=ALU.mult)
            S.dma('sp', out=arow[:], in_=hy_a_log[0, :].partition_broadcast(128))
            S.op('act', 'activation', out=arow[:], in_=arow[:], func=AF.Exp)
            S.op('dve', 'tensor_scalar', out=arow[:], in0=arow[:], scalar1=-1.0, scalar2=None, op0=ALU.mult)
            S.dma('sp', out=dskr[:], in_=hy_d_skip[0, :].partition_broadcast(128))
            S.barrier(mark, 'ph_P0'); chk('P0')

        def fe_load_sub(es_bufs, src_d, t0, s):
            xs2, hn, hxT, ss, rstd, psTt = es_bufs
            xs_ = xs2[s % 2]
            S.dma('sp', out=xs_[:], in_=src_d[t0 + s * 128:t0 + (s + 1) * 128, :])
            S.memset('pool', ss[:, s:s + 1], 0.0)
            S.op('act', 'activation', out=hn[:, s, :], in_=xs_[:], func=AF.Square, accum_out=ss[:, s:s + 1])
            S.op('dve', 'tensor_scalar', out=rstd[:, s:s + 1], in0=ss[:, s:s + 1], scalar1=1.0 / D, scalar2=EPS,
                 op0=ALU.mult, op1=ALU.add)
            S.op('act', 'activation', out=rstd[:, s:s + 1], in_=rstd[:, s:s + 1], func=AF.Sqrt)
            S.op('dve', 'reciprocal', out=rstd[:, s:s + 1], in_=rstd[:, s:s + 1])
            if s % 2 == 0:
                S.op('dve', 'tensor_scalar', out=hn[:, s, :], in0=xs_[:], scalar1=rstd[:, s:s + 1], scalar2=None,
                     op0=ALU.mult)
            else:
                S.op('act', 'activation', out=hn[:, s, :], in_=xs_[:], func=AF.Identity, scale=rstd[:, s:s + 1])

        def fe_load(es_bufs, src_d, t0, ns):
            for s in range(ns):
                fe_load_sub(es_bufs, src_d, t0, s)

        def fe_trans(es_bufs, ns, Acol, Bcol, par):
            xs2, hn, hxT, ss, rstd, psTt = es_bufs
            hx_ = hxT[par % len(hxT)]
            for k in range(8):
                pt = psTt[k % len(psTt)]
                for s in range(ns):
                    S.op('pe', 'transpose', out=pt[:, s * 128:(s + 1) * 128], in_=hn[:, s, k * 128:(k + 1) * 128],
                         identity=ident[:])
                S.op('act', 'activation', out=hx_[:, k, 0:ns * 128], in_=pt[:, 0:ns * 128], func=AF.Identity,
                     bias=Bcol[:, k:k + 1], scale=Acol[:, k:k + 1])
            return hx_

        def front_end(es_bufs, src_d, t0, ns, Acol, Bcol, par):
            fe_load(es_bufs, src_d, t0, ns)
            return fe_trans(es_bufs, ns, Acol, Bcol, par)

        def fe_bufs(es, nsmax=4, nhx=2, npt=2):
            xs2 = [sb(es, "xs%d" % i, [128, D]) for i in range(2)]
            hn = sb(es, "hn", [128, nsmax, D], BF16)
            hxT = [sb(es, "hxT%d" % i, [128, 8, nsmax * 128], BF16) for i in range(nhx)]
            ss = sb(es, "ss", [128, 4])
            rstd = sb(es, "rstd", [128, 4])
            psTt = [ps(es, "psTt%d" % i, [128, 1024], BF16) for i in range(npt)]
            return (xs2, hn, hxT, ss, rstd, psTt)

        with mid:
            S.memset('pool', Vt[:], 1.0)

            with contextlib.ExitStack() as es:
                win = sb(es, "win", [128, 8, INP], BF16)
                wsw = sb(es, "wsw", [128, 8, 640], BF16)
                wstage = [sb(es, "wstage%d" % i, [128, 640]) for i in range(2)]
                for k in range(8):
                    S.dma('pool', out=win.p(k)[:, k, :], in_=hy_w_in[0, k * 128:(k + 1) * 128, :])
                    st = wstage[k % 2]
                    S.dma('sp', out=st[:], in_=hy_w_in[0, k * 128:(k + 1) * 128, 0:640])
                    S.op('dve', 'tensor_copy', out=wsw.p(k)[:, k, 0:640:2], in_=st[:, 1:640:2])
                    S.op('dve', 'tensor_copy', out=wsw.p(k)[:, k, 1:640:2], in_=st[:, 0:640:2])
                dtb = sb(es, "dtb", [128, 16])
                S.dma('sp', out=dtb[:], in_=hy_dt_bias[0, :].partition_broadcast(128))
                feb = fe_bufs(es)
                psA = [ps(es, "psA%d" % i, [128, 512]) for i in range(4)]
                psM = ps(es, "psMs", [128, 512])
                sq = sb(es, "sq", [128, 512], BF16)
                rs = sb(es, "rs", [128, 512])
                ta = sb(es, "ta", [128, 512])
                tb = sb(es, "tb", [128, 512])
                Ct = sb(es, "Ct", [128, 512])
                St = sb(es, "St", [128, 512])
                qst = [sb(es, "qst%d" % i, [128, 4, 512], BF16) for i in range(2)]
                zst = [sb(es, "zst0", [128, 4, 512])] * 2
                xst = [sb(es, "xst0", [128, 6, 512])] * 2
                epsc = sb(es, "epsc", [128, 1])
                S.memset('pool', epsc[:], EPS)

                def qk_proj(hx_, n, col0, rope, slot):
                    pp, psw = psA[2 * slot], psA[2 * slot + 1]
                    for k in range(8):
                        S.op('pe', 'matmul', out=pp[:, 0:n], lhsT=win[:, k, col0:col0 + 128], rhs=hx_[:, k, 0:n],
                             start=(k == 0), stop=(k == 7))
                    if rope:
                        for k in range(8):
                            S.op('pe', 'matmul', out=psw[:, 0:n], lhsT=wsw[:, k, col0:col0 + 128], rhs=hx_[:, k, 0:n],
                                 start=(k == 0), stop=(k == 7))

                def qk_post(n, gi_, dst, rope, slot):
                    pp, psw = psA[2 * slot], psA[2 * slot + 1]
                    S.op('act', 'activation', out=sq[:, 0:n], in_=pp[:, 0:n], func=AF.Square)
                    S.op('pe', 'matmul', out=psM[:, 0:n], lhsT=blk[:], rhs=sq[:, 0:n], start=True, stop=True)
                    S.op('act', 'activation', out=rs[:, 0:n], in_=psM[:, 0:n], func=AF.Sqrt, bias=epsc[:, 0:1], scale=1.0)
                    S.op('dve', 'reciprocal', out=rs[:, 0:n], in_=rs[:, 0:n])
                    S.op('act', 'activation', out=ta[:, 0:n], in_=pp[:, 0:n], func=AF.Identity, scale=gcol[:, gi_:gi_ + 1])
                    if rope:
                        S.op('act', 'activation', out=tb[:, 0:n], in_=psw[:, 0:n], func=AF.Identity,
                             scale=gcol[:, gi_ + 1:gi_ + 2])
                        S.op('pool', 'tensor_tensor', out=ta[:, 0:n], in0=ta[:, 0:n], in1=Ct[:, 0:n], op=ALU.mult)
                        S.op('pool', 'tensor_tensor', out=tb[:, 0:n], in0=tb[:, 0:n], in1=St[:, 0:n], op=ALU.mult)
                        S.op('dve', 'tensor_tensor', out=ta[:, 0:n], in0=ta[:, 0:n], in1=tb[:, 0:n], op=ALU.add)
                    S.op('dve', 'tensor_tensor', out=dst, in0=ta[:, 0:n], in1=rs[:, 0:n], op=ALU.mult)

                def qk_pipeline(hx_, n, jobs):
                    for idx, (col0, gi_, dst, rope) in enumerate(jobs):
                        qk_proj(hx_, n, col0, rope, idx % 2)
                        if idx >= 1:
                            c0, g0, d0, r0_ = jobs[idx - 1]
                            qk_post(n, g0, d0, r0_, (idx - 1) % 2)
                    c0, g0, d0, r0_ = jobs[-1]
                    qk_post(n, g0, d0, r0_, (len(jobs) - 1) % 2)

                tiles = [('ctx', 0)] + [('lat', i) for i in range(NT)]
                for ti, (kind, i) in enumerate(tiles):
                    par = ti % 2
                    lat = kind == 'lat'
                    ns = 4 if lat else 2
                    n = ns * 128
                    t0 = i * 512
                    if lat:
                        hx_ = front_end(feb, x_in, t0, ns, cols[:, 0:8], cols[:, 8:16], par)
                    else:
                        hx_ = front_end(feb, ctx_in, 0, ns, colsc[:, 8:16], colsc[:, 0:8], par)
                    key0 = t0 if lat else T
                    kt0 = key0 // 128
                    if lat:
                        r0 = t0 // 64
                        S.op('pool', 'tensor_tensor', out=Ct[:].rearrange("p (r c) -> p r c", c=64),
                             in0=ropeT[:, 0, r0:r0 + 8].unsqueeze(2).to_broadcast([128, 8, 64]),
                             in1=ropeT[:, 2, 0:64].unsqueeze(1).to_broadcast([128, 8, 64]), op=ALU.mult)
                        S.op('pool', 'tensor_tensor', out=St[:].rearrange("p (r c) -> p r c", c=64),
                             in0=ropeT[:, 1, r0:r0 + 8].unsqueeze(2).to_broadcast([128, 8, 64]),
                             in1=ropeT[:, 3, 0:64].unsqueeze(1).to_broadcast([128, 8, 64]), op=ALU.add)
                        jobs = [(j * 128, 0, qst[par][:, j, :], True) for j in range(4)]
                        jobs.append((512, 2, KT[:, key0:key0 + n], True))
                        qk_pipeline(hx_, n, jobs)
                        S.dma('pool', out=qT_d[:, t0:t0 + 512].rearrange("(j p) t -> p j t", p=128), in_=qst[par][:])
                    if not lat:
                        qk_pipeline(hx_, n, [(512, 2, KT[:, key0:key0 + n], False)])
                    for j in range(6):
                        pp = psA[2 + j % 2]
                        for k in range(8):
                            S.op('pe', 'matmul', out=pp[:, 0:n], lhsT=win[:, k, 1280 + j * 128:1280 + (j + 1) * 128],
                                 rhs=hx_[:, k, 0:n], start=(k == 0), stop=(k == 7))
                        S.op('dve' if j % 2 == 0 else 'act', 'tensor_copy' if j % 2 == 0 else 'copy',
                             out=xst[par][:, j, 0:n], in_=pp[:, 0:n])
                    if lat:
                        S.dma('pool', out=xbc_d[:, t0:t0 + 512].rearrange("(j p) t -> p j t", p=128), in_=xst[par][:])
                    else:
                        S.dma('pool', out=xbcc_d[:, :].rearrange("(j p) t -> p j t", p=128), in_=xst[par][:, :, 0:n])
                    for s in range(ns):
                        pv = psA[0]
                        for k in range(8):
                            S.op('pe', 'matmul', out=pv[:, 0:128], lhsT=hx_[:, k, s * 128:(s + 1) * 128], rhs=win[:, k, 640:768],
                                 start=(k == 0), stop=(k == 7))
                        S.op('dve', 'tensor_copy', out=Vt[:, kt0 + s, :, 0:64],
                             in_=pv[:, 0:128].rearrange("p (a d) -> p a d", a=2))
                        pd = psA[1]
                        for k in range(8):
                            S.op('pe', 'matmul', out=pd[:, 0:16], lhsT=hx_[:, k, s * 128:(s + 1) * 128], rhs=win[:, k, 2048:2064],
                                 start=(k == 0), stop=(k == 7))
                        ch = (t0 // 128 + s) if lat else (NCH + s)
                        S.op('dve', 'tensor_tensor', out=dt_all[:, ch, :], in0=pd[:, 0:16], in1=dtb[:], op=ALU.add)
                        if lat:
                            pz = psA[2 + s % 2]
                            for k in range(8):
                                S.op('pe', 'matmul', out=pz[:], lhsT=hx_[:, k, s * 128:(s + 1) * 128], rhs=win[:, k, 768:1280],
                                     start=(k == 0), stop=(k == 7))
                            S.op('act', 'activation', out=zst[par][:, s, :], in_=pz[:], func=AF.Silu)
                    if lat:
                        S.dma('pool', out=zs_d[t0:t0 + 512, :].rearrange("(s p) c -> p s c", p=128), in_=zst[par][:])
                dtt = sb(es, "dtt", [128, NCH + 2, 16])
                S.op('act', 'activation', out=dtt[:], in_=dt_all[:], func=AF.Abs)
                S.op('act', 'activation', out=dtt[:], in_=dtt[:], func=AF.Exp, scale=-1.0)
                S.op('act', 'activation', out=dtt[:], in_=dtt[:], func=AF.Ln, bias=onec[:, 0:1], scale=1.0)
                S.op('dve', 'scalar_tensor_tensor', out=dt_all[:], in0=dt_all[:], scalar=0.0, in1=dtt[:], op0=ALU.max,
                     op1=ALU.add)
                S.barrier(mark, 'ph_A'); chk('A')

            with contextlib.ExitStack() as es:
                cw = sb(es, "cw", [128, 6, 3])
                cbias = sb(es, "cbias", [128, 6])
                for w_ in range(3):
                    S.dma('sp', out=cw[:, :, w_], in_=hy_conv_w[0, w_, :].rearrange("(j p) -> p j", p=128),
                          allow_slow_non_contiguous=True)
                S.dma('sp', out=cbias[:], in_=hy_conv_b[0, :].rearrange("(j p) -> p j", p=128), allow_slow_non_contiguous=True)
                dskb = sb(es, "dskb", [128, 512])
                S.op('dve', 'tensor_copy', out=dskb[:].rearrange("p (h d) -> p h d", h=8),
                     in_=dskr[:].unsqueeze(2).to_broadcast([128, 8, 64]))
                snr = sb(es, "snr", [128, 512])
                S.dma('sp', out=snr[:], in_=hy_ssm_norm[0, :].partition_broadcast(128))
                Xp = [sb(es, "Xp%d" % i, [128, 6, 514]) for i in range(2)]
                Xa = [sb(es, "Xa%d" % i, [128, 6, 512], BF16) for i in range(2)]
                CTm = [sb(es, "CTm%d" % i, [128, 2, 512], BF16) for i in range(2)]
                gmask = sb(es, "gmask", [128, 2])
                S.memset('pool', gmask[:], 0.0)
                S.memset('pool', gmask[0:64, 0:1], 1.0)
                S.memset('pool', gmask[64:128, 1:2], 1.0)
                cv = sb(es, "cv", [128, 512])
                psX = ps(es, "psX", [128, 8, 128], BF16)
                xtok = [sb(es, "xtok%d" % i_, [128, 640]) for i_ in range(2)]
                Btok = [sb(es, "Btok%d" % i_, [128, 128], BF16) for i_ in range(2)]
                dta = [sb(es, "dta%d" % i_, [128, 16]) for i_ in range(2)]
                psMisc = ps(es, "psMisc", [128, 512])
                dec = [sb(es, "dec%d" % i_, [128, 48]) for i_ in range(2)]
                xdt = [sb(es, "xdt%d" % i_, [128, 2, 512], BF16) for i_ in range(2)]
                xdd = [sb(es, "xdd%d" % i_, [128, 2, 512], BF16) for i_ in range(2)]
                psSt = ps(es, "psSt", [128, 2, 256])
                H = [sb(es, "H%d" % i, [128, 256]) for i in range(2)]
                Hb = [sb(es, "Hb%d" % i, [128, 256], BF16) for i in range(2)]
                gin_t = [sb(es, "gin%d" % i, [128, 256], BF16) for i in range(2)]
                rhsD = [sb(es, "rhsD%d" % i_, [128, 8, 128]) for i_ in range(2)]
                psDT = ps(es, "psDT", [128, 512])
                Ew = [sb(es, "Ew%d" % i_, [128, 1024], BF16) for i_ in range(2)]
                GM = [sb(es, "GM%d" % i_, [128, 2, 128], BF16) for i_ in range(2)]
                Wm = [[sb(es, "Wm%d_%d" % (c_, i), [128, 8, 128], BF16) for i in range(2)] for c_ in range(2)]
                psY = ps(es, "psY", [128, 512])
                psO = [ps(es, "psO%d" % i, [128, 512]) for i in range(2)]
                yb = [sb(es, "yb%d" % i_, [128, 512]) for i_ in range(2)]
                t1 = [sb(es, "t1%d" % i_, [128, 512]) for i_ in range(2)]
                zt = [sb(es, "zt%d" % i, [128, 4, 512]) for i in range(2)]
                gss = [sb(es, "gss%d" % i_, [128, 2]) for i_ in range(2)]
                junk2 = [sb(es, "junk2%d" % i_, [128, 256], BF16) for i_ in range(2)]
                yo = [sb(es, "yo%d" % i_, [128, 512], BF16) for i_ in range(2)]
                yst = [sb(es, "yst%d" % i, [128, 4, 512], BF16) for i in range(2)]
                for d_ in range(2):
                    S.memset('pool', H[d_][:], 0.0)
                    S.memset('pool', Hb[d_][:], 0.0)

                def prep_super(src_d, t0, n, tot, par):
                    X = Xp[par]
                    lo = max(t0 - 1, 0)
                    hi = min(t0 + n + 1, tot)
                    if t0 == 0:
                        S.memset('pool', X[:, :, 0:1], 0.0)
                    if t0 + n == tot:
                        S.memset('pool', X[:, :, n + 1:n + 2], 0.0)
                    S.dma('sp', out=X[:, :, 1 - (t0 - lo):1 + n + (hi - t0 - n)],
                          in_=src_d[:, lo:hi].rearrange("(j p) t -> p j t", p=128))
                    for j in range(6):
                        e1 = 'dve'
                        S.op(e1, 'tensor_scalar', out=cv[:, 0:n], in0=X[:, j, 1:n + 1], scalar1=cw[:, j, 1:2], scalar2=None,
                             op0=ALU.mult)
                        S.op(e1, 'scalar_tensor_tensor', out=cv[:, 0:n], in0=X[:, j, 0:n], scalar=cw[:, j, 0:1],
                             in1=cv[:, 0:n], op0=ALU.mult, op1=ALU.add)
                        S.op(e1, 'scalar_tensor_tensor', out=cv[:, 0:n], in0=X[:, j, 2:n + 2], scalar=cw[:, j, 2:3],
                             in1=cv[:, 0:n], op0=ALU.mult, op1=ALU.add)
                        S.op('act', 'activation', out=Xa[par][:, j, 0:n], in_=cv[:, 0:n], func=AF.Silu, bias=cbias[:, j:j + 1],
                             scale=1.0)

                def chunk_common(par, s, ch):
                    XA = Xa[par]
                    cb = ch % 2
                    for j in range(5):
                        S.op('pe', 'transpose', out=psX[:, j, :], in_=XA[:, j, s * 128:(s + 1) * 128], identity=ident[:])
                    S.op('act', 'copy', out=xtok[cb][:].rearrange("p (j c) -> p j c", j=5), in_=psX[:, 0:5, :])
                    S.op('pool', 'tensor_copy', out=Btok[cb][:], in_=xtok[cb][:, 512:640])
                    S.op('dve', 'tensor_tensor', out=dta[cb][:], in0=dt_all[:, ch, :], in1=arow[:], op=ALU.mult)
                    S.op('pe', 'matmul', out=psMisc[:, 256:264], lhsT=mk['le'][:], rhs=dta[cb][:, 0:8], start=True, stop=True)
                    S.op('pe', 'matmul', out=psMisc[:, 264:272], lhsT=mk['gt'][:], rhs=dta[cb][:, 0:8], start=True, stop=True)
                    S.op('pe', 'matmul', out=psMisc[:, 272:280], lhsT=mk['ge'][:], rhs=dta[cb][:, 8:16], start=True, stop=True)
                    S.op('pe', 'matmul', out=psMisc[:, 280:288], lhsT=mk['lt'][:], rhs=dta[cb][:, 8:16], start=True, stop=True)
                    S.op('pe', 'matmul', out=psMisc[:, 288:304], lhsT=ones_f[:], rhs=dta[cb][:], start=True, stop=True)
                    S.op('act', 'activation', out=dec[cb][:], in_=psMisc[:, 256:304], func=AF.Exp)

                def state_step(d_, ch, skip_xdt=False):
                    dcol = 8 if d_ == 0 else 24
                    cb = ch % 2
                    if not skip_xdt:
                        S.op('dve', 'tensor_tensor', out=xdt[cb][:, d_, :].rearrange("p (h d) -> p h d", h=8),
                             in0=xtok[cb][:, 0:512].rearrange("p (h d) -> p h d", h=8),
                             in1=dt_all[:, ch, d_ * 8:(d_ + 1) * 8].unsqueeze(2).to_broadcast([128, 8, 64]), op=ALU.mult)
                    S.op('pool', 'tensor_tensor', out=xdd[cb][:, d_, :].rearrange("p (h d) -> p h d", h=8),
                         in0=xdt[cb][:, d_, :].rearrange("p (h d) -> p h d", h=8),
                         in1=dec[cb][:, dcol:dcol + 8].unsqueeze(2).to_broadcast([128, 8, 64]), op=ALU.mult)
                    for g in range(2):
                        S.op('pe', 'matmul', out=psSt[:, g, :], lhsT=Btok[cb][:], rhs=xdd[cb][:, d_, g * 256:(g + 1) * 256], start=True,
                             stop=True)
                    for g in range(2):
                        hs = H[d_][g * 64:(g + 1) * 64, :]
                        S.op('dve', 'tensor_tensor', out=hs.rearrange("p (e d) -> p e d", e=4),
                             in0=hs.rearrange("p (e d) -> p e d", e=4),
                             in1=dec[cb][g * 64:(g + 1) * 64, 32 + d_ * 8 + g * 4:32 + d_ * 8 + g * 4 + 4].unsqueeze(2).to_broadcast(
                                 [64, 4, 64]), op=ALU.mult)
                        S.op('dve', 'tensor_tensor', out=hs, in0=hs, in1=psSt[g * 64:(g + 1) * 64, g, :], op=ALU.add)
                    S.op('pool', 'tensor_copy', out=Hb[d_][:], in_=H[d_][:])

                chkb('S0')
                prep_super(xbcc_d, 0, 256, 256, 0)
                chkb('Sp')
                for s in (0, 1):
                    chunk_common(0, s, NCH + s)
                    state_step(0, NCH + s)
                for s in (1, 0):
                    chunk_common(0, s, NCH + s)
                    state_step(1, NCH + s)
                chkb('Sc')
                def s1_prep(ch):
                    si, s = ch // 4, ch % 4
                    if s == 3:
                        prep_super(xbc_d, si * 512, 512, T, si % 2)
                    chunk_common(si % 2, s, ch)

                s1_prep(NCH - 1)
                for ch in range(NCH - 1, -1, -1):
                    if ch - 1 >= 0:
                        s1_prep(ch - 1)
                    gt_ = gin_t[ch % 2]
                    S.op('act', 'copy', out=gt_[:], in_=Hb[1][:])
                    S.dma('pool', out=gin_d.p(ch)[ch, :, :], in_=gt_[:])
                    state_step(1, ch)
                chkb('S1')
                def s2_stage1(ch):
                    si, s = ch // 4, ch % 4
                    par = si % 2
                    if s == 0:
                        prep_super(xbc_d, si * 512, 512, T, par)
                        for g in range(2):
                            S.op('act', 'activation', out=CTm[par][:, g, :], in_=Xa[par][:, 5, :], func=AF.Identity,
                                 scale=gmask[:, g:g + 1])
                        S.dma('sp', out=zt[par][:], in_=zs_d[si * 512:(si + 1) * 512, :].rearrange("(s p) c -> p s c", p=128))
                    cb = ch % 2
                    XA = Xa[par]
                    gt_ = gin_t[ch % 2]
                    S.dma('sp', out=gt_[:], in_=gin_d.p(ch)[ch, :, :])
                    chunk_common(par, s, ch)
                    for g in range(2):
                        S.op('pe', 'matmul', out=psMisc[:, g * 128:(g + 1) * 128], lhsT=XA[:, 4, s * 128:(s + 1) * 128],
                             rhs=CTm[par][:, g, s * 128:(s + 1) * 128], start=True, stop=True)
                    for d_ in range(2):
                        m1 = mk['le'] if d_ == 0 else mk['ge']
                        m2 = mk['gt'] if d_ == 0 else mk['lt']
                        S.op('pool', 'tensor_tensor', out=rhsD[d_][:],
                             in0=m1[:].unsqueeze(1).to_broadcast([128, 8, 128]),
                             in1=dta[cb][:, d_ * 8:(d_ + 1) * 8].unsqueeze(2).to_broadcast([128, 8, 128]), op=ALU.mult)
                        for hh in range(2):
                            S.op('pe', 'matmul', out=psDT[:], lhsT=m2[:],
                                 rhs=rhsD[d_][:, hh * 4:(hh + 1) * 4, :].rearrange("p h l -> p (h l)"), start=True, stop=True)
                            S.op('act', 'activation', out=Ew[d_][:, hh * 512:(hh + 1) * 512], in_=psDT[:], func=AF.Exp)
                        S.op('dve', 'tensor_tensor', out=GM[d_][:], in0=psMisc[:, 0:256].rearrange("p (g l) -> p g l", g=2), in1=m1[:].unsqueeze(1).to_broadcast([128, 2, 128]),
                             op=ALU.mult)
                        S.op('dve' if d_ == 0 else 'pool', 'tensor_tensor',
                             out=Wm[cb][d_][:].rearrange("p (g e) l -> p g e l", g=2),
                             in0=Ew[d_][:].rearrange("p (g e l) -> p g e l", g=2, e=4),
                             in1=GM[d_][:].unsqueeze(2).to_broadcast([128, 2, 4, 128]), op=ALU.mult)
                        S.op('dve', 'tensor_tensor', out=xdt[cb][:, d_, :].rearrange("p (h d) -> p h d", h=8),
                             in0=xtok[cb][:, 0:512].rearrange("p (h d) -> p h d", h=8),
                             in1=dt_all[:, ch, d_ * 8:(d_ + 1) * 8].unsqueeze(2).to_broadcast([128, 8, 64]), op=ALU.mult)

                def s2_stage2(ch):
                    si, s = ch // 4, ch % 4
                    par = si % 2
                    cb = ch % 2
                    XA = Xa[par]
                    gt_ = gin_t[ch % 2]
                    for h in range(8):
                        for d_ in range(2):
                            S.op('pe', 'matmul', out=psY[:, h * 64:(h + 1) * 64], lhsT=Wm[cb][d_][:, h, :],
                                 rhs=xdt[cb][:, d_, h * 64:(h + 1) * 64], start=(d_ == 0), stop=(d_ == 1))
                    for d_ in range(2):
                        st_ = Hb[0] if d_ == 0 else gt_
                        for g in range(2):
                            S.op('pe', 'matmul', out=psO[d_][:, g * 256:(g + 1) * 256],
                                 lhsT=CTm[par][:, g, s * 128:(s + 1) * 128], rhs=st_[:, :],
                                 start=True, stop=True)
                    S.op('pool', 'tensor_tensor', out=t1[cb][:], in0=xtok[cb][:, 0:512], in1=dskb[:], op=ALU.mult)
                    S.op('dve', 'tensor_tensor', out=yb[cb][:], in0=psY[:], in1=t1[cb][:], op=ALU.add)
                    for d_ in range(2):
                        ecol = 0 if d_ == 0 else 16
                        S.op('dve', 'tensor_tensor', out=t1[cb][:].rearrange("p (h d) -> p h d", h=8),
                             in0=psO[d_][:].rearrange("p (h d) -> p h d", h=8),
                             in1=dec[cb][:, ecol:ecol + 8].unsqueeze(2).to_broadcast([128, 8, 64]), op=ALU.mult)
                        S.op('dve', 'tensor_tensor', out=yb[cb][:], in0=yb[cb][:], in1=t1[cb][:], op=ALU.add)
                    state_step(0, ch, skip_xdt=True)
                    S.op('dve', 'tensor_tensor', out=yb[cb][:], in0=yb[cb][:], in1=zt[par][:, s, :], op=ALU.mult)
                    S.memset('pool', gss[cb][:], 0.0)
                    for g in range(2):
                        S.op('act', 'activation', out=junk2[cb][:], in_=yb[cb][:, g * 256:(g + 1) * 256], func=AF.Square,
                             accum_out=gss[cb][:, g:g + 1])
                    S.op('dve', 'tensor_scalar', out=gss[cb][:], in0=gss[cb][:], scalar1=1.0 / 256, scalar2=EPS, op0=ALU.mult,
                         op1=ALU.add)
                    S.op('act', 'activation', out=gss[cb][:], in_=gss[cb][:], func=AF.Sqrt)
                    S.op('dve', 'reciprocal', out=gss[cb][:], in_=gss[cb][:])
                    S.op('dve', 'tensor_tensor', out=yb[cb][:].rearrange("p (g c) -> p g c", g=2),
                         in0=yb[cb][:].rearrange("p (g c) -> p g c", g=2),
                         in1=gss[cb][:].unsqueeze(2).to_broadcast([128, 2, 256]), op=ALU.mult)
                    S.op('pool', 'tensor_tensor', out=yo[cb][:], in0=yb[cb][:], in1=snr[:], op=ALU.mult)
                    for j in range(4):
                        S.op('pe', 'transpose', out=psX[:, j, :], in_=yo[cb][:, j * 128:(j + 1) * 128], identity=ident[:])
                    S.op('act', 'copy', out=yst[par][:, :, s * 128:(s + 1) * 128], in_=psX[:, 0:4, :])
                    if s == 3:
                        S.dma('pool', out=ysT_d[:, si * 512:(si + 1) * 512].rearrange("(j p) t -> p j t", p=128), in_=yst[par][:])

                s2_stage1(0)
                for ch in range(NCH):
                    if ch + 1 < NCH:
                        s2_stage1(ch + 1)
                    s2_stage2(ch)

                S.barrier(mark, 'ph_S'); chk('S')

            with contextlib.ExitStack() as es:
                grow = sb(es, "grow", [128, D])
                S.dma('sp', out=grow[:], in_=grow_d[0, :].partition_broadcast(128))
                woA = sb(es, "woA", [64, 8, D], BF16)
                woB = sb(es, "woB", [128, 4, D], BF16)
                wst = [sb(es, "wst%d" % i, [128, D]) for i in range(2)]
                for h in range(8):
                    st = wst[h % 2]
                    S.dma('sp', out=st[0:64, :], in_=hy_w_out[0, h * 64:(h + 1) * 64, :])
                    S.op('dve', 'tensor_tensor', out=woA.p(h)[:, h, :], in0=st[0:64, :], in1=grow[0:64, :], op=ALU.mult)
                for j in range(4):
                    st = wst[j % 2]
                    S.dma('sp', out=st[:], in_=hy_w_out[0, 512 + j * 128:512 + (j + 1) * 128, :])
                    S.op('dve', 'tensor_tensor', out=woB.p(j)[:, j, :], in0=st[:], in1=grow[:], op=ALU.mult)
                qt = [sb(es, "qt%d" % i, [128, 4, 512], BF16) for i in range(2)]
                ysl = [sb(es, "ysl%d" % i, [128, 4, 512], BF16) for i in range(2)]
                xr = [sb(es, "xr%d" % i, [128, D]) for i in range(2)]
                PT = [sb(es, "PT%d" % i, [128, 1024], BF16) for i in range(3)]
                psSc = [ps(es, "psSc%d" % i, [128, 1024]) for i in range(3)]
                psOa = [ps(es, "psOa0", [65, 512])] * 2
                psMo = [ps(es, "psMo0", [128, 512])] * 2
                Osb = [sb(es, "Osb%d" % i, [128, 512]) for i in range(2)]
                for o_ in Osb:
                    S.memset('pool', o_[:], 0.0)
                rec = sb(es, "rec", [64, 512])
                mixT = [sb(es, "mixT%d" % i, [64, 8, 512], BF16) for i in range(2)]
                LOOK = 2
                assert NKT % 2 == 0
                NKP = NKT // 2
                iters = [(i, h, kp) for i in range(NT) for h in range(8) for kp in range(NKP)]
                pending = []

                def load_tile(i):
                    par = i % 2
                    t0 = i * 512
                    for kv in range(2):
                        S.dma('sp', out=qt[par][kv * 64:(kv + 1) * 64, :, :],
                              in_=qT_d[kv * 256:(kv + 1) * 256, t0:t0 + 512].rearrange("(e d) t -> d e t", d=64))
                    S.dma('sp', out=ysl[par][:], in_=ysT_d[:, t0:t0 + 512].rearrange("(j p) t -> p j t", p=128))

                def rec_score(n):
                    i, h, kp = iters[n]
                    if h == 0 and kp == 0:
                        load_tile(i)
                    kv, e = h // 4, h % 4
                    for u in range(2):
                        kt = 2 * kp + u
                        S.op('pe', 'matmul', out=psSc[n % 3][:, u * 512:(u + 1) * 512],
                             lhsT=KT[kv * 64:(kv + 1) * 64, kt * 128:(kt + 1) * 128],
                             rhs=qt[i % 2][kv * 64:(kv + 1) * 64, e, :], start=True, stop=True)

                def mk_final(i, h):
                    def f():
                        ob = Osb[h % 2]
                        S.op('pe', 'matmul', out=psMo[1][:, :], lhsT=bsel[:], rhs=ob[:], start=True, stop=True)
                        S.op('dve', 'reciprocal', out=rec[:], in_=psMo[1][0:64, :])
                        S.op('dve', 'tensor_tensor', out=mixT[i % 2][:, h, :], in0=ob[0:64, :], in1=rec[:], op=ALU.mult)
                    return f

                def mk_outproj(i, s, nh):
                    def f():
                        par = i % 2
                        t0 = i * 512
                        xr_ = xr[s % 2]
                        if nh == 0:
                            S.dma('sp', out=xr_[:], in_=x_in[t0 + s * 128:t0 + (s + 1) * 128, :])
                        pm = psMo[0]
                        for h in range(8):
                            S.op('pe', 'matmul', out=pm[:], lhsT=mixT[par][:, h, s * 128:(s + 1) * 128],
                                 rhs=woA[:, h, nh * 512:(nh + 1) * 512], start=(h == 0), stop=False)
                        for j in range(4):
                            S.op('pe', 'matmul', out=pm[:], lhsT=ysl[par][:, j, s * 128:(s + 1) * 128],
                                 rhs=woB[:, j, nh * 512:(nh + 1) * 512], start=False, stop=(j == 3))
                        S.op('dve', 'tensor_tensor', out=xr_[:, nh * 512:(nh + 1) * 512], in0=pm[:],
                             in1=xr_[:, nh * 512:(nh + 1) * 512], op=ALU.add)
                        if nh == 1:
                            S.dma('pool', out=x1_d[t0 + s * 128:t0 + (s + 1) * 128, :], in_=xr_[:])
                    return f

                for n in range(min(LOOK, len(iters))):
                    rec_score(n)
                since_final = 0
                for n, (i, h, kp) in enumerate(iters):
                    if n + LOOK < len(iters):
                        rec_score(n + LOOK)
                    kv = h // 4
                    po = psOa[h % 2]
                    S.op('act', 'activation', out=PT[n % 3][:], in_=psSc[n % 3][:], func=AF.Exp, scale=0.125)
                    for u in range(2):
                        kt = 2 * kp + u
                        S.op('pe', 'matmul', out=po[:], lhsT=Vt[:, kt, kv, :], rhs=PT[n % 3][:, u * 512:(u + 1) * 512],
                             start=(kt == 0), stop=(kt == NKT - 1))
                    if kp == NKP - 1:
                        S.op('dve', 'tensor_copy', out=Osb[h % 2][0:65, :], in_=po[:])
                        pending.append((n + 2, mk_final(i, h)))
                        if h == 7:
                            for s in range(4):
                                for nh in range(2):
                                    pending.append((n + 3, mk_outproj(i, s, nh)))
                    if pending and pending[0][0] <= n:
                        pending.pop(0)[1]()
                while pending:
                    pending.pop(0)[1]()
                S.barrier(mark, 'ph_B1'); chk('B1')

        def ffn_phase(l, src_d, dst_d, final):
            with contextlib.ExitStack() as es:
                wg = sb(es, "wg", [128, 8, FH], BF16)
                wu = sb(es, "wu", [128, 8, FH], BF16)
                wd = sb(es, "wd", [128, NJ, D], BF16)
                with contextlib.ExitStack() as es2:
                    grow = sb(es2, "growf", [128, D])
                    wst = [sb(es2, "wstf%d" % i, [128, D]) for i in range(2)]
                    S.dma('sp', out=grow[:], in_=grow_d[l * 2 + 1, :].partition_broadcast(128))
                    for k in range(8):
                        S.dma('pool', out=wg.p(k)[:, k, :], in_=w_gate[l, k * 128:(k + 1) * 128, :])
                        S.dma('pool', out=wu.p(k)[:, k, :], in_=w_up[l, k * 128:(k + 1) * 128, :])
                    for j in range(NJ):
                        st = wst[j % 2]
                        S.dma('sp', out=st[:], in_=w_down[l, j * 128:(j + 1) * 128, :])
                        S.op('dve' if j % 2 == 0 else 'pool', 'tensor_tensor', out=wd.p(j)[:, j, :], in0=st[:], in1=grow[:],
                             op=ALU.mult)
                    S.barrier(mark, 'ph_FW'); chk('FW')
                feb = fe_bufs(es, nhx=1)
                Acol = cols[:, l * 32 + 16:l * 32 + 24]
                Bcol = cols[:, l * 32 + 24:l * 32 + 32]
                aT = sb(es, "aT", [128, NJ, 512], BF16)
                sg = [sb(es, "sg%d" % i, [128, 512]) for i in range(2)]
                xr = [sb(es, "xrf%d" % i, [128, D]) for i in range(2)]
                psGa = [ps(es, "psGa%d" % i, [128, 512]) for i in range(2)]
                psUa = [ps(es, "psUa%d" % i, [128, 512]) for i in range(2)]
                psD = [ps(es, "psD%d" % i, [128, 512]) for i in range(2)]
                if final:
                    fnr = sb(es, "fnr", [128, D])
                    S.dma('sp', out=fnr[:], in_=final_norm[:].partition_broadcast(128))
                    fss = sb(es, "fss", [128, 1])
                    fjunk = sb(es, "fjunk", [128, D], BF16)
                fe_load(feb, src_d, 0, 4)
                hx_ = fe_trans(feb, 4, Acol, Bcol, 0)
                for i in range(NT):
                    t0 = i * 512
                    for j in range(NJ):
                        pg, pu = psGa[j % 2], psUa[j % 2]
                        for k in range(8):
                            S.op('pe', 'matmul', out=pg[:], lhsT=wg[:, k, j * 128:(j + 1) * 128], rhs=hx_[:, k, :],
                                 start=(k == 0), stop=(k == 7))
                        for k in range(8):
                            S.op('pe', 'matmul', out=pu[:], lhsT=wu[:, k, j * 128:(j + 1) * 128], rhs=hx_[:, k, :],
                                 start=(k == 0), stop=(k == 7))
                        S.op('act', 'activation', out=sg[j % 2][:], in_=pg[:], func=AF.Silu)
                        S.op('dve', 'tensor_tensor', out=aT[:, j, :], in0=sg[j % 2][:], in1=pu[:], op=ALU.mult)
                        if i + 1 < NT and 1 <= j <= 4:
                            fe_load_sub(feb, src_d, t0 + 512, j - 1)
                    if i + 1 < NT:
                        hx_ = fe_trans(feb, 4, Acol, Bcol, 0)
                    for s in range(4):
                        xr_ = xr[s % 2]
                        S.dma('pool', out=xr_[:], in_=src_d[t0 + s * 128:t0 + (s + 1) * 128, :])
                        for nh in range(2):
                            pd_ = psD[nh]
                            for j in range(NJ):
                                S.op('pe', 'matmul', out=pd_[:], lhsT=aT[:, j, s * 128:(s + 1) * 128],
                                     rhs=wd[:, j, nh * 512:(nh + 1) * 512], start=(j == 0), stop=(j == NJ - 1))
                            S.op('dve', 'tensor_tensor', out=xr_[:, nh * 512:(nh + 1) * 512], in0=pd_[:],
                                 in1=xr_[:, nh * 512:(nh + 1) * 512], op=ALU.add)
                        if final:
                            S.memset('pool', fss[:], 0.0)
                            S.op('act', 'activation', out=fjunk[:], in_=xr_[:], func=AF.Square, accum_out=fss[:, 0:1])
                            S.op('dve', 'tensor_scalar', out=fss[:], in0=fss[:], scalar1=1.0 / D, scalar2=EPS, op0=ALU.mult,
                                 op1=ALU.add)
                            S.op('act', 'activation', out=fss[:], in_=fss[:], func=AF.Sqrt)
                            S.op('dve', 'reciprocal', out=fss[:], in_=fss[:])
                            S.op('dve', 'scalar_tensor_tensor', out=xr_[:], in0=xr_[:], scalar=fss[:, 0:1],
                                 in1=fnr[:], op0=ALU.mult, op1=ALU.mult)
                        S.dma('pool', out=dst_d[t0 + s * 128:t0 + (s + 1) * 128, :], in_=xr_[:])
                S.barrier(mark, 'ph_F'); chk('F')

        ffn_phase(0, x1_d, x2_d, False)

        with contextlib.ExitStack() as es:
            wi = sb(es, "wi", [128, 8, 3 * D], BF16)
            wo = sb(es, "wo", [128, 8, D], BF16)
            scw = sb(es, "scw", [128, 8, 3])
            with contextlib.ExitStack() as es2:
                grow = sb(es2, "growc", [128, D])
                wst = [sb(es2, "wstc%d" % i, [128, D]) for i in range(2)]
                S.dma('sp', out=grow[:], in_=grow_d[2, :].partition_broadcast(128))
                for w_ in range(3):
                    S.dma('sp', out=scw[:, :, w_], in_=sc_conv_w[0, w_, :].rearrange("(j p) -> p j", p=128),
                          allow_slow_non_contiguous=True)
                for k in range(8):
                    S.dma('pool', out=wi.p(k)[:, k, :], in_=sc_w_in[0, k * 128:(k + 1) * 128, :])
                    st = wst[k % 2]
                    S.dma('sp', out=st[:], in_=sc_w_out[0, k * 128:(k + 1) * 128, :])
                    S.op('dve', 'tensor_tensor', out=wo.p(k)[:, k, :], in0=st[:], in1=grow[:], op=ALU.mult)
                S.barrier(mark, 'ph_CW'); chk('CW')
            feb = fe_bufs(es, npt=1)
            gbt = [sb(es, "gbt%d" % i, [128, 8, 512]) for i in range(2)]
            vt = [sb(es, "vt%d" % i, [128, 8, 514]) for i in range(2)]
            gcs = [sb(es, "gcs%d" % i, [128, 512]) for i in range(2)]
            cvc = [sb(es, "cvc%d" % i, [128, 512]) for i in range(2)]
            wT = sb(es, "wT", [128, 8, 512], BF16)
            xr = [sb(es, "xrc%d" % i, [128, D]) for i in range(2)]
            psC = [[ps(es, "psC%d_%d" % (a_, i), [128, 512]) for i in range(3)] for a_ in range(2)]
            psMc = ps(es, "psMc", [128, 512])
            Acol = cols[:, 32:40]
            Bcol = cols[:, 40:48]

            def conv_chunk(pq, j):
                Vv = vt[pq]
                cv_ = cvc[j % 2]
                S.op('dve', 'tensor_scalar', out=cv_[:], in0=Vv[:, j, 1:513], scalar1=scw[:, j, 1:2], scalar2=None,
                     op0=ALU.mult)
                S.op('dve', 'scalar_tensor_tensor', out=cv_[:], in0=Vv[:, j, 0:512], scalar=scw[:, j, 0:1], in1=cv_[:],
                     op0=ALU.mult, op1=ALU.add)
                S.op('dve', 'scalar_tensor_tensor', out=cv_[:], in0=Vv[:, j, 2:514], scalar=scw[:, j, 2:3], in1=cv_[:],
                     op0=ALU.mult, op1=ALU.add)
                S.op('pool', 'tensor_tensor', out=wT[:, j, :], in0=cv_[:], in1=gbt[pq][:, j, :], op=ALU.mult)

            def out_proj(tp):
                for s in range(4):
                    xr_ = xr[s % 2]
                    S.dma('sp', out=xr_[:], in_=x2_d[tp + s * 128:tp + (s + 1) * 128, :])
                    for nh in range(2):
                        pm = psMc
                        for k in range(8):
                            S.op('pe', 'matmul', out=pm[:], lhsT=wT[:, k, s * 128:(s + 1) * 128],
                                 rhs=wo[:, k, nh * 512:(nh + 1) * 512], start=(k == 0), stop=(k == 7))
                        S.op('dve', 'tensor_tensor', out=xr_[:, nh * 512:(nh + 1) * 512], in0=pm[:],
                             in1=xr_[:, nh * 512:(nh + 1) * 512], op=ALU.add)
                    S.dma('pool', out=x3_d[tp + s * 128:tp + (s + 1) * 128, :], in_=xr_[:])

            fe_load(feb, x2_d, 0, 4)
            hx_ = fe_trans(feb, 4, Acol, Bcol, 0)
            for i in range(NT):
                par = i % 2
                if i + 1 < NT:
                    fe_load(feb, x2_d, (i + 1) * 512, 4)
                for j in range(8):
                    pb_, pc_, pu_ = psC[j % 2]
                    for (pp_, cbase) in ((pc_, D), (pu_, 2 * D), (pb_, 0)):
                        for k in range(8):
                            S.op('pe', 'matmul', out=pp_[:], lhsT=wi[:, k, cbase + j * 128:cbase + (j + 1) * 128],
                                 rhs=hx_[:, k, :], start=(k == 0), stop=(k == 7))
                    S.op('act', 'copy', out=gcs[j % 2][:], in_=pc_[:])
                    S.op('dve', 'tensor_tensor', out=vt[par][:, j, 1:513], in0=gcs[j % 2][:], in1=pu_[:], op=ALU.mult)
                    S.op('act', 'copy', out=gbt[par][:, j, :], in_=pb_[:])
                    if i == 0:
                        S.memset('pool', vt[par][:, j, 0:1], 0.0)
                    else:
                        S.op('pool', 'tensor_copy', out=vt[par][:, j, 0:1], in_=vt[1 - par][:, j, 512:513])
                        S.op('pool', 'tensor_copy', out=vt[1 - par][:, j, 513:514], in_=vt[par][:, j, 1:2])
                        conv_chunk(1 - par, j)
                if i + 1 < NT:
                    hx_ = fe_trans(feb, 4, Acol, Bcol, i + 1)
                if i >= 1:
                    out_proj((i - 1) * 512)
            lp = (NT - 1) % 2
            for j in range(8):
                S.memset('pool', vt[lp][:, j, 513:514], 0.0)
                conv_chunk(lp, j)
            out_proj((NT - 1) * 512)
            S.barrier(mark, 'ph_C1'); chk('C1')

        ffn_phase(1, x3_d, out, True)
        S.flush()
    except _Stop:
        pass
    return nc, S


_CACHE = {}


def _consts():
    p = np.arange(128)
    i = (p % 64) // 2
    inv = 1.0 / (10000.0 ** ((2.0 * (i % 16)) / 32.0))
    fr = np.where(i < 16, inv, 0.0)
    fc = np.where(i >= 16, inv, 0.0)
    sign = np.where(p % 2 == 0, -1.0, 1.0)
    kc = np.stack([fr, fc, sign, np.zeros(128)], axis=1).astype(np.float32)
    return kc, np.arange(128, dtype=np.float32)


def kernel(**inputs):
    x = np.ascontiguousarray(inputs['x'], dtype=np.float32)
    B, T, _ = x.shape
    import os
    stop = os.environ.get("K_STOP")
    if (T, stop) not in _CACHE:
        _CACHE[(T, stop)] = build(T, stop)
    nc, _ = _CACHE[(T, stop)]
    kc, pos = _consts()
    shared = {}
    for k in ('ada_w', 'ada_b', 'norm_mix', 'norm_ffn', 'ffn_w_gate', 'ffn_w_up', 'ffn_w_down', 'hy_w_in', 'hy_q_norm',
              'hy_k_norm', 'hy_conv_w', 'hy_conv_b', 'hy_d_skip', 'hy_ssm_norm', 'hy_w_out', 'sc_w_in', 'sc_conv_w',
              'sc_w_out', 'final_norm', 'c_ctx'):
        shared[k] = np.ascontiguousarray(inputs[k], dtype=np.float32)
    shared['hy_dt_bias'] = np.ascontiguousarray(inputs['hy_dt_bias'], dtype=np.float32).reshape(1, 16)
    shared['hy_a_log'] = np.ascontiguousarray(inputs['hy_a_log'], dtype=np.float32).reshape(1, 16)
    shared['kconst'] = kc
    shared['kpos'] = pos
    in_maps = []
    for b in range(B):
        m = dict(shared)
        m['x'] = x[b]
        m['c'] = np.ascontiguousarray(inputs['c'][b], dtype=np.float32)
        m['ctx'] = np.ascontiguousarray(inputs['ctx'][b], dtype=np.float32)
        in_maps.append(m)
    res = run_bass_kernel_spmd(nc, in_maps, core_ids=list(range(B)))
    return np.stack([res.results[b]['out'] for b in range(B)], axis=0).astype(np.float32)
```
